# Optimizing a Trainium2 kernel written in Bass

```python
import numpy as np
import jax
import jax.numpy as jnp
from jax import lax

D_MODEL = 2048
BATCH = 4
SEQ = 4096
DEPTH = 2

HEAD_DIM = 64
MIX_WIDTH = D_MODEL
RW_WIDTH = MIX_WIDTH // 2
AT_WIDTH = MIX_WIDTH - RW_WIDTH
RW_HEADS = RW_WIDTH // HEAD_DIM
AT_HEADS = AT_WIDTH // HEAD_DIM
GQA_GROUP = 4
AT_KV_HEADS = AT_HEADS // GQA_GROUP
KV_WIDTH = AT_KV_HEADS * HEAD_DIM
DECAY_RANK = max(32, int(round(1.8 * D_MODEL ** 0.5 / 32)) * 32)
ICLR_RANK = DECAY_RANK
WINDOW = 128
BLOCK = 128
ROPE_THETA = 10000.0
NORM_EPS = 1e-6
LNX_EPS = 64e-5
RW_SHIFT_WIDTH = 3 * RW_WIDTH + DECAY_RANK + ICLR_RANK
IN_COLS = RW_SHIFT_WIDTH + RW_WIDTH + AT_WIDTH + 2 * KV_WIDTH + AT_WIDTH

kernel_name = "hybrid_rwkv7_swa_sink_encoder"


def rms_norm(x, g):
    xf = x.astype(jnp.float32)
    y = xf * lax.rsqrt(jnp.mean(xf * xf, axis=-1, keepdims=True) + NORM_EPS)
    return (y * g.astype(jnp.float32)).astype(x.dtype)


def bidir_token_shift(u, mu):
    prev = jnp.pad(u[:, :-1], ((0, 0), (1, 0), (0, 0)))
    nxt = jnp.pad(u[:, 1:], ((0, 0), (0, 1), (0, 0)))
    return u + mu[0] * (prev - u) + mu[1] * (nxt - u)


def rope(t, positions):
    half = HEAD_DIM // 2
    inv = jnp.power(ROPE_THETA, -jnp.arange(half, dtype=jnp.float32) / half)
    ang = positions.astype(jnp.float32)[:, :, None] * inv
    cos = jnp.cos(ang)[:, :, None, :]
    sin = jnp.sin(ang)[:, :, None, :]
    tf = t.astype(jnp.float32)
    t1, t2 = tf[..., :half], tf[..., half:]
    return jnp.concatenate([t1 * cos - t2 * sin, t2 * cos + t1 * sin], axis=-1)


def wkv_scan(r, decay, k, v, kk, a, reverse):
    B, T, H, N = r.shape

    def step(S, inp):
        r_t, w_t, k_t, v_t, kk_t, a_t = inp
        s_kk = jnp.einsum('bhvk,bhk->bhv', S, kk_t)
        S = (S * w_t[:, :, None, :]
             - s_kk[..., None] * (a_t * kk_t)[:, :, None, :]
             + v_t[..., None] * k_t[:, :, None, :])
        return S, jnp.einsum('bhvk,bhk->bhv', S, r_t)

    xs = tuple(jnp.moveaxis(t, 1, 0) for t in (r, decay, k, v, kk, a))
    S0 = jnp.zeros((B, H, N, N), jnp.float32)
    _, y = lax.scan(step, S0, xs, reverse=reverse)
    return jnp.moveaxis(y, 0, 1)


def rwkv7_branch(r, k, v, xw, xa, w0, decay_up, a0, iclr_up, k_k, k_a, r_k, lnx_g, lnx_b):
    B, T, _ = r.shape
    f32 = jnp.float32
    heads = lambda t: t.astype(f32).reshape(B, T, RW_HEADS, HEAD_DIM)
    rf, kf, vf = r.astype(f32), k.astype(f32), v.astype(f32)
    kk = heads(kf * k_k.astype(f32))
    kk = kk * lax.rsqrt(jnp.maximum(jnp.sum(kk * kk, -1, keepdims=True), 1e-24))
    lw = jnp.tanh(xw.astype(f32))
    xaf = xa.astype(f32)
    rh, vh = heads(rf), heads(vf)
    rkf = r_k.astype(f32)
    y = jnp.zeros_like(rh)
    bonus = jnp.zeros_like(rh[..., :1])
    for d, rev in enumerate((False, True)):
        w = -jax.nn.softplus(-(w0[d].astype(f32) + lw @ decay_up[d].astype(f32))) - 0.5
        decay = jnp.exp(-jnp.exp(w))
        a = jax.nn.sigmoid(a0[d].astype(f32) + xaf @ iclr_up[d].astype(f32))
        kd = heads(kf * (1.0 + (a - 1.0) * k_a.astype(f32)))
        y = y + wkv_scan(rh, heads(decay), kd, vh, kk, heads(a), rev)
        bonus = bonus + jnp.sum(rh * kd * rkf, -1, keepdims=True)
    mu = jnp.mean(y, -1, keepdims=True)
    var = jnp.mean(jnp.square(y - mu), -1, keepdims=True)
    yn = (y - mu) * lax.rsqrt(var + LNX_EPS)
    yn = yn * lnx_g.astype(f32).reshape(RW_HEADS, HEAD_DIM) + lnx_b.astype(f32).reshape(RW_HEADS, HEAD_DIM)
    out = yn + bonus * vh
    return out.reshape(B, T, RW_WIDTH).astype(r.dtype)


def window_attention(q, k, v, sink):
    B, T, _, Dh = q.shape
    nb = T // BLOCK
    qb = (q * (Dh ** -0.5)).reshape(B, nb, BLOCK, AT_KV_HEADS, GQA_GROUP, Dh)

    def band(t):
        tp = jnp.pad(t, ((0, 0), (BLOCK, BLOCK), (0, 0), (0, 0)))
        tp = tp.reshape(B, nb + 2, BLOCK, AT_KV_HEADS, Dh)
        return jnp.concatenate([tp[:, :-2], tp[:, 1:-1], tp[:, 2:]], axis=2)

    kb, vb = band(k), band(v)
    s = jnp.einsum('bnqhgd,bnkhd->bnhgqk', qb, kb).astype(jnp.float32)
    qi = jnp.arange(nb)[:, None, None] * BLOCK + jnp.arange(BLOCK)[None, :, None]
    kj = jnp.arange(nb)[:, None, None] * BLOCK - BLOCK + jnp.arange(3 * BLOCK)[None, None, :]
    mask = (jnp.abs(kj - qi) <= WINDOW) & (kj >= 0) & (kj < T)
    s = jnp.where(mask[None, :, None, None], s, -jnp.inf)
    sk = sink.astype(jnp.float32).reshape(AT_KV_HEADS, GQA_GROUP)[None, None, :, :, None, None]
    m = jnp.maximum(jnp.max(s, -1, keepdims=True), sk)
    p = jnp.exp(s - m)
    denom = jnp.sum(p, -1, keepdims=True) + jnp.exp(sk - m)
    o = jnp.einsum('bnhgqk,bnkhd->bnqhgd', p / denom, vb.astype(jnp.float32))
    return o.reshape(B, T, AT_HEADS * Dh)


def setup_inputs(seed: int = 0) -> dict:
    key = jax.random.key(seed)
    ks = jax.random.split(key, 18)
    f32 = jnp.float32
    nrm = lambda k, shape, s: jax.random.normal(k, shape, f32) * s
    x = nrm(ks[0], (BATCH, SEQ, D_MODEL), 1.0)
    positions = (jnp.arange(SEQ, dtype=jnp.int32)[None, :]
                 + jax.random.randint(ks[1], (BATCH, 1), 0, 1024, dtype=jnp.int32))
    norm_g = 1.0 + nrm(ks[2], (DEPTH, D_MODEL), 0.02)
    w_in = nrm(ks[3], (DEPTH, D_MODEL, IN_COLS), D_MODEL ** -0.5)
    shift_mu = jax.random.uniform(ks[4], (DEPTH, 2, RW_SHIFT_WIDTH), f32, 0.0, 0.5)
    profile = jnp.tile(-6.5 + 5.0 * jnp.linspace(0.0, 1.0, HEAD_DIM, dtype=f32) ** 0.85, RW_HEADS)
    w0 = profile + nrm(ks[5], (DEPTH, 2, RW_WIDTH), 0.1)
    decay_up = nrm(ks[6], (DEPTH, 2, DECAY_RANK, RW_WIDTH), 0.5 * DECAY_RANK ** -0.5)
    a0 = nrm(ks[7], (DEPTH, 2, RW_WIDTH), 0.1)
    iclr_up = nrm(ks[8], (DEPTH, 2, ICLR_RANK, RW_WIDTH), 0.5 * ICLR_RANK ** -0.5)
    k_k = 0.85 + nrm(ks[9], (DEPTH, RW_WIDTH), 0.02)
    k_a = 1.0 + nrm(ks[10], (DEPTH, RW_WIDTH), 0.02)
    r_k = nrm(ks[11], (DEPTH, RW_HEADS, HEAD_DIM), 0.1)
    lnx_g = 1.0 + nrm(ks[12], (DEPTH, RW_WIDTH), 0.02)
    lnx_b = nrm(ks[13], (DEPTH, RW_WIDTH), 0.02)
    sink = nrm(ks[14], (DEPTH, AT_HEADS), 0.5)
    w_out = nrm(ks[15], (DEPTH, MIX_WIDTH, D_MODEL), MIX_WIDTH ** -0.5)
    final_g = 1.0 + nrm(ks[16], (D_MODEL,), 0.02)
    return {"x": x, "positions": positions, "norm_g": norm_g, "w_in": w_in, "shift_mu": shift_mu,
            "w0": w0, "decay_up": decay_up, "a0": a0, "iclr_up": iclr_up, "k_k": k_k, "k_a": k_a,
            "r_k": r_k, "lnx_g": lnx_g, "lnx_b": lnx_b, "sink": sink, "w_out": w_out,
            "final_g": final_g}


def reference(x, positions, norm_g, w_in, shift_mu, w0, decay_up, a0, iclr_up, k_k, k_a, r_k,
              lnx_g, lnx_b, sink, w_out, final_g):
    B, T, _ = x.shape
    col_splits = np.cumsum([RW_SHIFT_WIDTH, RW_WIDTH, AT_WIDTH, KV_WIDTH, KV_WIDTH]).tolist()
    rw_splits = np.cumsum([RW_WIDTH, RW_WIDTH, RW_WIDTH, DECAY_RANK]).tolist()
    for l in range(DEPTH):
        h = rms_norm(x, norm_g[l])
        proj = jnp.einsum('btd,dc->btc', h, w_in[l])
        rw_in, g_rw, q, k_at, v_at, g_at = jnp.split(proj, col_splits, axis=-1)
        rw_in = bidir_token_shift(rw_in, shift_mu[l])
        r, k, v, xw, xa = jnp.split(rw_in, rw_splits, axis=-1)
        y_rw = rwkv7_branch(r, k, v, xw, xa, w0[l], decay_up[l], a0[l], iclr_up[l],
                            k_k[l], k_a[l], r_k[l], lnx_g[l], lnx_b[l])
        qh = rope(q.reshape(B, T, AT_HEADS, HEAD_DIM), positions)
        kh = rope(k_at.reshape(B, T, AT_KV_HEADS, HEAD_DIM), positions)
        vh = v_at.reshape(B, T, AT_KV_HEADS, HEAD_DIM)
        y_at = window_attention(qh, kh, vh, sink[l]).astype(x.dtype)
        mix = jnp.concatenate([y_rw * jax.nn.silu(g_rw), y_at * jax.nn.silu(g_at)], axis=-1)
        x = x + jnp.einsum('btc,cd->btd', mix, w_out[l])
    return rms_norm(x, final_g)
```

```python
from concourse.bass_utils import run_bass_kernel_spmd
import concourse.bass as bass
import concourse.mybir as mybir

ENGS = ("pe", "act", "dve", "pool", "sp")


class Buf:
    __slots__ = ("name", "writers", "readers")

    def __init__(self, name):
        self.name = name
        self.writers = []
        self.readers = []


class Op:
    __slots__ = ("eng", "fn", "reads", "writes", "dma_key", "deps", "signal", "cnt", "idx", "phase_i", "slot")

    def __init__(self, eng, fn, reads, writes, dma_key):
        self.eng, self.fn, self.reads, self.writes, self.dma_key = eng, fn, reads, writes, dma_key
        self.deps = set()
        self.signal = False
        self.cnt = None


class _Rec:
    def __init__(self):
        self.call = None

    def __getattr__(self, name):
        def f(*a, **k):
            self.call = (name, a, k)
            return None
        return f


def _bind(fn):
    r = _Rec()
    fn(r)
    name, a, k = r.call
    return lambda e: getattr(e, name)(*a, **k)


class Prog:
    def __init__(self, nc, same_engine_sync=True):
        self.nc = nc
        self.ops = []
        self.same_engine_sync = same_engine_sync
        self.final_dma = []
        self.phase = Buf("phase")
        self.phase_i = 0

    def buf(self, name):
        return Buf(name)

    def op(self, eng, fn, reads=(), writes=()):
        o = Op(eng, _bind(fn), tuple(reads) + (self.phase,), tuple(writes), None)
        o.phase_i = self.phase_i
        self.ops.append(o)
        return o

    def barrier(self, fn):
        o = Op("dve", _bind(fn), (), (self.phase,), None)
        o.phase_i = self.phase_i
        self.phase_i += 1
        self.ops.append(o)
        return o

    def dma(self, eng, fn, key, reads=(), writes=(), final=False):
        o = Op(eng, _bind(fn), tuple(reads) + (self.phase,), tuple(writes), key)
        o.phase_i = self.phase_i
        self.ops.append(o)
        if final:
            self.final_dma.append(o)
        return o

    def emit(self, stack):
        nc = self.nc
        ops = self.ops
        for i, o in enumerate(ops):
            o.idx = i
        for o in ops:
            for b in o.reads:
                for w in b.writers:
                    o.deps.add(w)
            for b in o.writes:
                for w in b.writers:
                    o.deps.add(w)
                for r in b.readers:
                    o.deps.add(r)
            for b in o.reads:
                b.readers.append(o)
            for b in o.writes:
                if b.readers:
                    b.writers = [o]
                    b.readers = []
                else:
                    b.writers.append(o)
            o.deps.discard(o)
            best = {}
            for dd in o.deps:
                k_ = ("k", dd.dma_key, dd.phase_i) if dd.dma_key is not None else ("e", dd.eng)
                if k_ not in best or dd.idx > best[k_].idx:
                    best[k_] = dd
            o.deps = set(best.values())
        for o in ops:
            for d in o.deps:
                if d.dma_key is not None:
                    continue
                if d.eng == o.eng and (d.eng == "pe" or not self.same_engine_sync) and o.dma_key is None:
                    continue
                d.signal = True
        eng_cnt = {e: 0 for e in ENGS}
        slot_cnt = []
        key_slot = {}
        phase_nslots = {}
        slot_of = {}
        for o in ops:
            if o.dma_key is not None:
                cls = 1 if o.eng == "pool" else 0
                ks = (o.dma_key, o.phase_i, cls)
                if ks not in key_slot:
                    n = phase_nslots.get((o.phase_i, cls), 0)
                    phase_nslots[(o.phase_i, cls)] = n + 1
                    if (cls, n) not in slot_of:
                        slot_of[(cls, n)] = len(slot_cnt)
                        slot_cnt.append(0)
                    key_slot[ks] = slot_of[(cls, n)]
                o.slot = key_slot[ks]
                slot_cnt[o.slot] += 16
                o.cnt = slot_cnt[o.slot]
            elif o.signal:
                eng_cnt[o.eng] += 1
                o.cnt = eng_cnt[o.eng]
        sems = {}
        for e in ENGS:
            if eng_cnt[e] > 0:
                sems[e] = stack.enter_context(nc.semaphore("s_" + e))
        ksems = {}
        for n in range(len(slot_cnt)):
            ksems[n] = stack.enter_context(nc.semaphore(f"d_slot{n}"))
        self.n_sems = len(sems) + len(ksems)
        block = stack.enter_context(nc.Block())
        per_eng = {e: [o for o in ops if o.eng == e] for e in ENGS}
        final_dma = self.final_dma

        def gen(ename):
            def body(e):
                waited = {}
                for o in per_eng[ename]:
                    need = {}
                    for d in o.deps:
                        if d.dma_key is not None:
                            s = ("k", d.slot)
                        else:
                            if d.eng == o.eng and (d.eng == "pe" or not self.same_engine_sync) and o.dma_key is None:
                                continue
                            s = ("e", d.eng)
                        if d.cnt > need.get(s, 0):
                            need[s] = d.cnt
                    for s, v in need.items():
                        if waited.get(s, 0) >= v:
                            continue
                        waited[s] = v
                        sem = ksems[s[1]] if s[0] == "k" else sems[s[1]]
                        e.wait_ge(sem, v)
                    ins = o.fn(e)
                    if o.dma_key is not None:
                        ins.then_inc(ksems[o.slot], 16)
                    elif o.signal:
                        ins.then_inc(sems[o.eng], 1)
                fin = {}
                for o in final_dma:
                    if o.eng == ename:
                        fin[o.slot] = max(fin.get(o.slot, 0), o.cnt)
                for k, v in fin.items():
                    e.wait_ge(ksems[k], v)
            return body

        if per_eng["pe"]:
            block.tensor(gen("pe"))
        if per_eng["act"]:
            block.scalar(gen("act"))
        if per_eng["dve"]:
            block.vector(gen("dve"))
        if per_eng["pool"]:
            block.gpsimd(gen("pool"))
        if per_eng["sp"]:
            block.sync(gen("sp"))
        self.stats = {e: len(per_eng[e]) for e in ENGS}
        self.stats["sems"] = self.n_sems
        self.stats["signals"] = dict(eng_cnt)


import os
import numpy as np
from contextlib import ExitStack
import concourse.bass as bass
import concourse.mybir as mybir

F32 = mybir.dt.float32
BF16 = mybir.dt.bfloat16
I32 = mybir.dt.int32
AF = mybir.ActivationFunctionType
ALU = mybir.AluOpType

D = 2048
T = 4096
NT = 29
NCOL = NT * 128
KC = D // 128
T_R, T_K, T_V, T_XW, T_XA, T_GRW, T_Q, T_KA, T_KB, T_VAT, T_GAT = 0, 4, 8, 12, 13, 14, 18, 22, 23, 24, 25
NPAR = 128


class Ctx:
    pass


def emit_inproj(P, c, nc, sb, ps, x_d, wsel_d, projT_d, par, ident_b, tt=T, nt=NT):
    HT = 2048 if tt >= 2048 else tt
    nhalf = tt // HT
    xt = [sb(f"xt{i}", [128, D], F32) for i in range(2)]
    xb = [sb(f"xb{i}", [128, D], BF16) for i in range(2)]
    junk = sb("junk", [128, D], BF16)
    stat = [sb(f"stat{i}", [128, 2], F32) for i in range(2)]
    hT = sb("hT", [128, KC, HT], BF16)
    wt = [sb(f"wt{i}", [128, KC, 128], BF16) for i in range(3)]
    stg = [sb(f"stg{i}", [128, 512], F32) for i in range(3)]
    pT = [ps(f"pT{i}", [128, D], BF16) for i in range(2)]
    pm = [ps(f"pm{i}", [128, 512], F32) for i in range(2)]
    b_xt = [P.buf(f"xt{i}") for i in range(2)]
    b_xb = [P.buf(f"xb{i}") for i in range(2)]
    b_junk = P.buf("junk")
    b_stat = [P.buf(f"stat{i}") for i in range(2)]
    b_hT = [P.buf(f"hT{i}") for i in range(HT // 128)]
    b_wt = [P.buf(f"wt{i}") for i in range(3)]
    b_stg = [P.buf(f"stg{i}") for i in range(3)]
    b_pT = [P.buf(f"pT{i}") for i in range(2)]
    b_pm = [P.buf(f"pm{i}") for i in range(2)]
    gcol = par["g"]
    epsc = par["eps"]
    b_par = par["buf"]
    wv = wsel_d.rearrange("(kc p) c -> p kc c", p=128)
    nev = 0
    for half in range(nhalf):
        for ti in range(HT // 128):
            i = ti % 2
            tok0 = half * HT + ti * 128
            P.dma("sp", lambda e, i=i, tok0=tok0: e.dma_start(out=xt[i][:], in_=x_d[tok0:tok0 + 128, :]), b_xt[i], writes=[b_xt[i]])
            P.op("dve", lambda e, i=i: e.memset(stat[i][:], 0.0), writes=[b_stat[i]])
            P.op("act", lambda e, i=i: e.activation(out=junk[:], in_=xt[i][:], func=AF.Square, accum_out=stat[i][:, 0:1]),
                 reads=[b_xt[i]], writes=[b_junk, b_stat[i]])
            P.op("act", lambda e, i=i: e.activation(out=stat[i][:, 1:2], in_=stat[i][:, 0:1], func=AF.Sqrt, scale=1.0 / D, bias=epsc),
                 reads=[b_stat[i], b_par], writes=[b_stat[i]])
            P.op("dve", lambda e, i=i: e.reciprocal(out=stat[i][:, 1:2], in_=stat[i][:, 1:2]), reads=[b_stat[i]], writes=[b_stat[i]])
            P.op("dve", lambda e, i=i: e.tensor_scalar(out=xb[i][:], in0=xt[i][:], scalar1=stat[i][:, 1:2], scalar2=None, op0=ALU.mult),
                 reads=[b_xt[i], b_stat[i]], writes=[b_xb[i]])
            for kc in range(KC):
                P.op("pe", lambda e, i=i, kc=kc: e.transpose(out=pT[i][:, kc * 128:(kc + 1) * 128], in_=xb[i][:, kc * 128:(kc + 1) * 128], identity=ident_b),
                     reads=[b_xb[i], c.b_const], writes=[b_pT[i]])
            eng = "dve" if ti % 2 == 0 else "pool"
            eng = "dve"
            P.op(eng, lambda e, i=i, ti=ti: e.tensor_tensor(
                out=hT[:, :, ti * 128:(ti + 1) * 128],
                in0=pT[i][:].rearrange("p (k t) -> p k t", k=KC),
                in1=gcol.unsqueeze(2).to_broadcast([128, KC, 128]), op=ALU.mult),
                reads=[b_pT[i], b_par], writes=[b_hT[ti]])
        for j in range(nt):
            wi = j % 3
            P.dma("pool", lambda e, wi=wi, j=j: e.dma_start(out=wt[wi][:], in_=wv[:, :, j * 128:(j + 1) * 128]), b_wt[wi], writes=[b_wt[wi]])
            for tg in range(HT // 512):
                pi = nev % 2
                si = nev % 3
                for kc in range(KC):
                    P.op("pe", lambda e, pi=pi, wi=wi, kc=kc, tg=tg: e.matmul(pm[pi][:], lhsT=wt[wi][:, kc, :], rhs=hT[:, kc, tg * 512:(tg + 1) * 512],
                                                                          start=(kc == 0), stop=(kc == KC - 1)),
                         reads=[b_wt[wi]] + b_hT[tg * 4:(tg + 1) * 4], writes=[b_pm[pi]])
                if nev % 2 == 0:
                    P.op("act", lambda e, pi=pi, si=si: e.copy(out=stg[si][:], in_=pm[pi][:]), reads=[b_pm[pi]], writes=[b_stg[si]])
                else:
                    P.op("dve", lambda e, pi=pi, si=si: e.tensor_copy(out=stg[si][:], in_=pm[pi][:]), reads=[b_pm[pi]], writes=[b_stg[si]])
                t0 = half * HT + tg * 512
                P.dma("sp", lambda e, si=si, j=j, t0=t0: e.dma_start(out=projT_d[j * 128:(j + 1) * 128, t0:t0 + 512], in_=stg[si][:]),
                      b_stg[si], reads=[b_stg[si]], writes=[c.b_projT[j]])
                nev += 1


def emit_derived(P, par_sb, b_par):
    def c0(dst, m0, m1):
        P.op("dve", lambda e: e.tensor_tensor(out=par_sb[:, dst:dst + 1], in0=par_sb[:, m0:m0 + 1], in1=par_sb[:, m1:m1 + 1], op=ALU.add), reads=[b_par], writes=[b_par])
        P.op("dve", lambda e: e.tensor_scalar(out=par_sb[:, dst:dst + 1], in0=par_sb[:, dst:dst + 1], scalar1=-1.0, scalar2=1.0, op0=ALU.mult, op1=ALU.add), reads=[b_par], writes=[b_par])
    for i in range(4):
        pb = i * 15
        c0(96 + i * 4 + 0, pb + 0, pb + 1)
        c0(96 + i * 4 + 1, pb + 2, pb + 3)
        c0(96 + i * 4 + 2, pb + 4, pb + 5)
        P.op("dve", lambda e, i=i, pb=pb: e.tensor_scalar(out=par_sb[:, 96 + i * 4 + 3:96 + i * 4 + 4], in0=par_sb[:, pb + 11:pb + 12], scalar1=-1.0, scalar2=1.0, op0=ALU.mult, op1=ALU.add),
             reads=[b_par], writes=[b_par])
    c0(112, 60, 61)
    c0(113, 62, 63)


def build_A(stage=1, tt=T, tiles=(0, 1, 2, 3), dirs=(0, 1)):
    nc = bass.Bass("TRN2", target_bir_lowering=False)
    c = Ctx()
    x_d = nc.dram_tensor("x", [tt, D], F32, kind="ExternalInput").ap()
    wsel_d = nc.dram_tensor("wsel", [D, NCOL], F32, kind="ExternalInput").ap()
    par_d = nc.dram_tensor("par", [128, NPAR], F32, kind="ExternalInput").ap()
    cst_d = nc.dram_tensor("cst", [128, NCST, 128], F32, kind="ExternalInput").ap()
    lora_d = nc.dram_tensor("lora", [128, 4, 512], F32, kind="ExternalInput").ap()
    projT_d = nc.dram_tensor("projT", [NCOL, tt], F32, kind="ExternalOutput" if stage == 1 and tt < 4096 else "Internal").ap()
    yT_d = nc.dram_tensor("yT", [512, tt], F32, kind="ExternalOutput" if stage == 2 else "Internal").ap()
    pos_d = nc.dram_tensor("pos", [1, tt], I32, kind="ExternalInput").ap()
    mix_d = nc.dram_tensor("mix", [1024, tt], F32, kind="ExternalOutput" if stage >= 3 else "Internal").ap()
    with ExitStack() as st:
        sb = lambda name, shape, dt: st.enter_context(nc.sbuf_tensor(name, shape, dt))
        P = Prog(nc)
        c.b_projT = [P.buf(f"projT{j}") for j in range(NT)]
        c.b_yT = [P.buf(f"yT{j}") for j in range(4)]
        c.b_const = P.buf("const")
        c.b_mix = P.buf("mix")
        scr = sb("scr", [128, 1], F32)
        par_sb = sb("par_sb", [128, NPAR], F32)
        cst_sb = sb("cst_sb", [128, NCST, 128], F32)
        lora_sb = sb("lora_sb", [128, 4, 512], F32)
        ident_b = sb("ident_b", [128, 128], BF16)
        b_par = P.buf("par")
        b_lora = P.buf("lora")
        P.dma("sp", lambda e: e.dma_start(out=par_sb[:], in_=par_d[:, :]), b_par, writes=[b_par])
        P.dma("sp", lambda e: e.dma_start(out=cst_sb[:], in_=cst_d[:, :, :]), c.b_const, writes=[c.b_const])
        P.dma("sp", lambda e: e.dma_start(out=lora_sb[:], in_=lora_d[:, :, :]), b_lora, writes=[b_lora])
        P.op("dve", lambda e: e.tensor_copy(out=ident_b[:], in_=cst_sb[:, 0, :]), reads=[c.b_const], writes=[c.b_const])
        emit_derived(P, par_sb, b_par)
        par = {"buf": b_par, "g": par_sb[:, 64:80], "eps": par_sb[:, 80:81]}
        with ExitStack() as st1:
            sb1 = lambda name, shape, dt: st1.enter_context(nc.sbuf_tensor(name, shape, dt))
            ps1 = lambda name, shape, dt: st1.enter_context(nc.psum_tensor(name, shape, dt))
            emit_inproj(P, c, nc, sb1, ps1, x_d, wsel_d, projT_d, par, ident_b[:], tt=tt)
        if stage == 1:
            for o in P.ops:
                if o.dma_key is not None and any(w in c.b_projT for w in o.writes):
                    P.final_dma.append(o)
        if stage >= 2:
            P.barrier(lambda e: e.memset(scr[:], 0.0))
            with ExitStack() as st2:
                sb2 = lambda name, shape, dt: st2.enter_context(nc.sbuf_tensor(name, shape, dt))
                ps2 = lambda name, shape, dt: st2.enter_context(nc.psum_tensor(name, shape, dt))
                emit_wkv(P, c, nc, sb2, ps2, projT_d, par_sb, b_par, cst_sb, lora_sb, b_lora, yT_d, tt=tt, tiles=tiles, dirs=dirs)
            if stage == 2:
                for o in P.ops:
                    if o.dma_key is not None and any(w in c.b_yT for w in o.writes):
                        P.final_dma.append(o)
        if stage >= 3:
            P.barrier(lambda e: e.memset(scr[:], 0.0))
            with ExitStack() as st3:
                emit_attn(P, c, nc, st3, projT_d, yT_d, pos_d, mix_d, par_sb, b_par, cst_sb, scr[:], tt=tt)
        P.emit(st)
        print(P.stats)
    return nc


def col_segments(hh):
    segs = []
    for base in (0, 1024, 2048):
        for i in range(4):
            segs.append([(base + hh * 512 + i * 128, 128)])
    segs.append([(3072, 96)])
    segs.append([(3168, 96)])
    for i in range(4):
        segs.append([(3264 + hh * 512 + i * 128, 128)])
    for i in range(4):
        segs.append([(4288 + hh * 512 + i * 128, 128)])
    ka = 5312 + (2 * hh) * 64
    segs.append([(ka, 64), (ka, 64)])
    segs.append([(ka + 64, 64), (ka + 64, 64)])
    segs.append([(5568 + 2 * hh * 64, 128)])
    for i in range(4):
        segs.append([(5824 + hh * 512 + i * 128, 128)])
    assert len(segs) == NT
    return segs


def make_wsel(w_in_l, hh):
    out = np.zeros((D, NCOL), np.float32)
    for j, sg in enumerate(col_segments(hh)):
        o = j * 128
        for (s, w) in sg:
            out[:, o:o + w] = w_in_l[:, s:s + w]
            o += w
    return out


def make_consts():
    cst = np.zeros((128, NCST, 128), np.float32)
    ii = np.arange(128)
    same = (ii[:, None] // 64) == (ii[None, :] // 64)
    MUS = ((ii[:, None] < ii[None, :]) & same).astype(np.float32)
    MLS = ((ii[:, None] > ii[None, :]) & same).astype(np.float32)
    MUI = ((ii[:, None] <= ii[None, :]) & same).astype(np.float32)
    MLI = ((ii[:, None] >= ii[None, :]) & same).astype(np.float32)
    cst[:, 0] = np.eye(128)
    cst[:, 1], cst[:, 2], cst[:, 3], cst[:, 4] = MUS, MLS, MUI, MLI
    cst[:, 5] = same.astype(np.float32)
    R = np.zeros((128, 128), np.float32)
    for dp in range(128):
        hb, o = dp // 64, dp % 64
        if o < 32:
            R[hb * 64 + o + 32, dp] = -1.0
        else:
            R[hb * 64 + o - 32, dp] = 1.0
    cst[:, 6] = R
    cst[:, 8], cst[:, 9] = -MUS, -MLS
    cst[:, 10], cst[:, 11] = -MLS, -MUS
    cst[:, 12], cst[:, 13] = MUS, MUI
    cst[:, 14], cst[:, 15] = MLS, MLI
    cst[:, 20] = (ii[:, None] >= ii[None, :]).astype(np.float32)
    cst[:, 21] = (ii[:, None] <= ii[None, :]).astype(np.float32)
    cm = np.ones(512, np.float32); cm[::64] = 0.0
    cst[:, 16:20, :] = cm.reshape(1, 4, 128)
    return cst


def make_par(inp, l, hh):
    par = np.zeros((128, NPAR), np.float32)
    mu = inp["shift_mu"][l]
    for i in range(4):
        ch = hh * 512 + i * 128
        pb = i * 15
        for k_, base in enumerate((0, 1024, 2048)):
            par[:, pb + 2 * k_] = mu[0, base + ch:base + ch + 128]
            par[:, pb + 2 * k_ + 1] = mu[1, base + ch:base + ch + 128]
        for d in range(2):
            par[:, pb + 6 + d] = inp["w0"][l, d, ch:ch + 128]
            par[:, pb + 8 + d] = inp["a0"][l, d, ch:ch + 128]
        par[:, pb + 10] = inp["k_k"][l, ch:ch + 128]
        par[:, pb + 11] = inp["k_a"][l, ch:ch + 128]
        par[:, pb + 12] = inp["r_k"][l].reshape(-1)[ch:ch + 128]
        par[:, pb + 13] = inp["lnx_g"][l, ch:ch + 128]
        par[:, pb + 14] = inp["lnx_b"][l, ch:ch + 128]
    par[:96, 60] = mu[0, 3072:3168]; par[:96, 61] = mu[1, 3072:3168]
    par[:96, 62] = mu[0, 3168:3264]; par[:96, 63] = mu[1, 3168:3264]
    par[:, 64:80] = inp["norm_g"][l].reshape(16, 128).T
    par[:, 80] = 1e-6
    par[:, 81] = 64e-5
    half = 32
    inv = np.power(np.float32(10000.0), -np.arange(half, dtype=np.float32) / np.float32(half)).astype(np.float32)
    par[:, 82] = inv[np.arange(128) % 32]
    sk = inp["sink"][l]
    for g in range(2):
        h0 = hh * 8 + 4 * g
        par[:, 84 + 4 * g:88 + 4 * g] = np.array([sk[h0], sk[h0 + 2], sk[h0 + 1], sk[h0 + 3]], np.float32)[None, :]
    return par


def make_lora(inp, l, hh):
    lo = np.zeros((128, 4, 512), np.float32)
    for d in range(2):
        lo[:96, d] = inp["decay_up"][l, d][:, hh * 512:(hh + 1) * 512]
        lo[:96, 2 + d] = inp["iclr_up"][l, d][:, hh * 512:(hh + 1) * 512]
    return lo


SEG = 512
C_DECAY = -0.6065306597126334
CI_ID, CI_MUS, CI_MLS, CI_MUI, CI_MLI, CI_ONES, CI_ROT, CI_CM = 0, 1, 2, 3, 4, 5, 6, 16
NCST = 22


def emit_wkv(P, c, nc, sb, ps, projT_d, par_sb, b_par, cst_sb, lora_sb, b_lora, yT_d, tt=T, tiles=(0, 1, 2, 3), dirs=(0, 1), tb=0, b_yT=None):
    nseg = tt // SEG
    NCH = SEG // 64
    f = lambda name, cols, dt=F32: sb(name, [128, cols], dt)
    raw = {u: f("raw_" + u, SEG + 2) for u in ("r", "k", "v", "xw", "xa")}
    sh = {u: f("sh_" + u, SEG) for u in ("r", "k", "v", "xw", "xa")}
    names = ["lw", "lg", "a", "kkr", "sq", "nrm", "kk", "kdm", "kd", "b", "F", "E", "Fi", "eFi", "eE", "enFi", "rho", "al", "be", "ka", "rkd", "tmp"]
    A = {n: f("w_" + n, SEG) for n in names}
    PC = f("w_PC", NCH)
    AB = {n: f("wb_" + n, SEG, BF16) for n in ("rho", "al", "be", "ka")}
    yacc = f("yacc", tt)
    bv = f("bv", tt)
    ST = f("ST", 64)
    LANES = 2
    NN = [[f(f"NN{l}_{k}", 256, BF16) for k in range(2)] for l in range(LANES)]
    PT = [f(f"PT{l}", 128, BF16) for l in range(LANES)]
    MM = [f(f"MM{l}", 384, BF16) for l in range(LANES)]
    tok5 = [f(f"tok5_{l}", 576, BF16) for l in range(LANES)]
    nW = [f(f"nW{l}", 320, BF16) for l in range(LANES)]
    QtTs = [[f(f"QtT{g}_{l}", 128) for l in range(LANES)] for g in range(2)]
    GpTs = [[f(f"GpT{g}_{l}", 128) for l in range(LANES)] for g in range(2)]
    QtT, GpT = QtTs[0], GpTs[0]
    bank = [ps(f"bank{l}", [128, 512], F32) for l in range(LANES)]
    alt = [ps(f"alt{l}", [128, 512], F32) for l in range(LANES)]
    Hss = [[f(f"Hs{g}_{l}", 128) for l in range(LANES)] for g in range(2)]
    Hs = Hss[0]
    pp = [ps(f"pp{k}", [128, 512], F32) for k in range(2)]
    pch = [ps(f"pch{k}", [128, 512], F32) for k in range(2)]
    B = lambda n: P.buf(n)
    b_raw = {u: B("raw_" + u) for u in raw}
    b_sh = {u: B("sh_" + u) for u in sh}
    bA = {n: B("w_" + n) for n in names}
    b_PC, b_yacc, b_bv, b_ST = B("PC"), B("yacc"), B("bv"), B("ST")
    bAB = {n: B("wb_" + n) for n in ("rho", "al", "be", "ka")}
    b_NN = [[B(f"NN{l}_{k}") for k in range(2)] for l in range(LANES)]
    b_PT = [B(f"PT{l}") for l in range(LANES)]
    b_MM = [B(f"MM{l}") for l in range(LANES)]
    b_tok5 = [B(f"tok5_{l}") for l in range(LANES)]
    b_nW = [B(f"nW{l}") for l in range(LANES)]
    b_QtTs = [[B(f"QtT{g}_{l}") for l in range(LANES)] for g in range(2)]
    b_GpTs = [[B(f"GpT{g}_{l}") for l in range(LANES)] for g in range(2)]
    b_QtT, b_GpT = b_QtTs[0], b_GpTs[0]
    b_bank = [B(f"bank{l}") for l in range(LANES)]
    b_alt = [B(f"alt{l}") for l in range(LANES)]
    b_Hss = [[B(f"Hs{g}_{l}") for l in range(LANES)] for g in range(2)]
    b_Hs = b_Hss[0]
    gctr = 0
    pend = []
    nchain = [0]
    b_pp = [B(f"pp{k}") for k in range(2)]
    b_pch = [B(f"pch{k}") for k in range(2)]
    bc = c.b_const
    cst = lambda i: cst_sb[:, i, :]
    pcol = lambda j: par_sb[:, j:j + 1]

    DEFER = [None]

    def _prebind(fn):
        r = _Rec()
        fn(r)
        name, a, k = r.call
        return lambda e: getattr(e, name)(*a, **k)

    def OP(eng, fn, reads, writes):
        if DEFER[0] is not None:
            fb_ = _prebind(fn)
            reads, writes = list(reads), list(writes)
            DEFER[0].append(lambda: P.op(eng, fb_, reads=reads, writes=writes))
        else:
            P.op(eng, fn, reads=reads, writes=writes)

    def DMA(eng, fn, key, reads, writes):
        if DEFER[0] is not None:
            fb_ = _prebind(fn)
            reads, writes = list(reads), list(writes)
            DEFER[0].append(lambda: P.dma(eng, fb_, key, reads=reads, writes=writes))
        else:
            P.dma(eng, fn, key, reads=reads, writes=writes)

    DBN = ("rho", "al", "be", "ka")
    A2 = {n: [A[n], f("w2_" + n, SEG)] for n in DBN}
    bA2 = {n: [bA[n], B("w2_" + n)] for n in DBN}
    AB2 = {n: [AB[n], f("wb2_" + n, SEG, BF16)] for n in DBN}
    bAB2 = {n: [bAB[n], B("wb2_" + n)] for n in DBN}
    shv2 = [sh["v"], f("sh2_v", SEG)]
    b_shv2 = [b_sh["v"], B("sh2_v")]
    PC2 = [PC, f("w2_PC", NCH)]
    b_PC2 = [b_PC, B("PC2")]
    segbase = [0]

    def bind_par(par_):
        nonlocal PC, b_PC
        for n_ in DBN:
            A[n_] = A2[n_][par_]
            bA[n_] = bA2[n_][par_]
            AB[n_] = AB2[n_][par_]
            bAB[n_] = bAB2[n_][par_]
        sh["v"] = shv2[par_]
        b_sh["v"] = b_shv2[par_]
        PC = PC2[par_]
        b_PC = b_PC2[par_]

    for l in range(LANES):
        OP("pool", lambda e, l=l: e.memset(tok5[l][:], 0.0), [], [b_tok5[l]])
        OP("pool", lambda e, l=l: e.memset(nW[l][:], 0.0), [], [b_nW[l]])
    V0, BE0, KA0, X10, AL0 = 64, 192, 320, 448, 512
    NW1, NW2 = 64, 192
    for i in tiles:
        pb = i * 15
        for d in dirs:
            OP("dve", lambda e: e.memset(ST[:], 0.0), [], [b_ST])
            segs = list(range(nseg)) if d == 0 else list(range(nseg - 1, -1, -1))
            def prep(s):
                t0 = s * SEG
                for u, trow in (("r", tb + T_R + i), ("k", tb + T_K + i), ("v", tb + T_V + i), ("xw", tb + T_XW), ("xa", tb + T_XA)):
                    lo = t0 - 1
                    hi = t0 + SEG + 1
                    dlo, dhi = 0, SEG + 2
                    if lo < 0:
                        lo, dlo = 0, 1
                    if hi > tt:
                        hi, dhi = tt, SEG + 1
                    if dlo == 1 or dhi == SEG + 1:
                        OP("pool", lambda e, u=u: e.memset(raw[u][:], 0.0), [], [b_raw[u]])
                    DMA("sp", lambda e, u=u, trow=trow, lo=lo, hi=hi, dlo=dlo, dhi=dhi: e.dma_start(
                        out=raw[u][:, dlo:dhi], in_=projT_d[trow * 128:(trow + 1) * 128, lo:hi]),
                        b_raw[u], [c.b_projT[trow]], [b_raw[u]])
                shp = {"r": (pb + 0, pb + 1, 96 + i * 4 + 0), "k": (pb + 2, pb + 3, 96 + i * 4 + 1), "v": (pb + 4, pb + 5, 96 + i * 4 + 2),
                       "xw": (60, 61, 112), "xa": (62, 63, 113)}
                for u in ("r", "k", "v", "xw", "xa"):
                    m0, m1, c0 = shp[u]
                    OP("act", lambda e, u=u, c0=c0: e.activation(out=sh[u][:], in_=raw[u][:, 1:SEG + 1], func=AF.Copy, scale=pcol(c0)),
                       [b_raw[u], b_par], [b_sh[u]])
                    OP("dve", lambda e, u=u, m0=m0: e.scalar_tensor_tensor(out=sh[u][:], in0=raw[u][:, 0:SEG], scalar=pcol(m0), in1=sh[u][:],
                                                                              op0=ALU.mult, op1=ALU.add), [b_raw[u], b_sh[u], b_par], [b_sh[u]])
                    OP("dve", lambda e, u=u, m1=m1: e.scalar_tensor_tensor(out=sh[u][:], in0=raw[u][:, 2:SEG + 2], scalar=pcol(m1), in1=sh[u][:],
                                                                              op0=ALU.mult, op1=ALU.add), [b_raw[u], b_sh[u], b_par], [b_sh[u]])
                OP("act", lambda e: e.activation(out=A["lw"][:], in_=sh["xw"][:], func=AF.Tanh), [b_sh["xw"]], [bA["lw"]])
                OP("pe", lambda e, d=d, i=i: e.matmul(pp[0][:], lhsT=lora_sb[0:96, d, i * 128:(i + 1) * 128], rhs=A["lw"][0:96, :], start=True, stop=True),
                   [bA["lw"], b_lora], [b_pp[0]])
                OP("act", lambda e, d=d: e.activation(out=A["lg"][:], in_=pp[0][:], func=AF.Sigmoid, bias=pcol(pb + 6 + d)), [b_pp[0], b_par], [bA["lg"]])
                OP("dve", lambda e: e.tensor_scalar(out=A["lg"][:], in0=A["lg"][:], scalar1=C_DECAY, scalar2=None, op0=ALU.mult), [bA["lg"]], [bA["lg"]])
                OP("pe", lambda e, d=d, i=i: e.matmul(pp[1][:], lhsT=lora_sb[0:96, 2 + d, i * 128:(i + 1) * 128], rhs=sh["xa"][0:96, :], start=True, stop=True),
                   [b_sh["xa"], b_lora], [b_pp[1]])
                OP("act", lambda e, d=d: e.activation(out=A["a"][:], in_=pp[1][:], func=AF.Sigmoid, bias=pcol(pb + 8 + d)), [b_pp[1], b_par], [bA["a"]])
                OP("dve", lambda e: e.tensor_scalar(out=A["kkr"][:], in0=sh["k"][:], scalar1=pcol(pb + 10), scalar2=None, op0=ALU.mult),
                   [b_sh["k"], b_par], [bA["kkr"]])
                OP("pool", lambda e: e.tensor_tensor(out=A["sq"][:], in0=A["kkr"][:], in1=A["kkr"][:], op=ALU.mult), [bA["kkr"]], [bA["sq"]])
                OP("pe", lambda e: e.matmul(pp[0][:], lhsT=cst(CI_ONES), rhs=A["sq"][:], start=True, stop=True), [bA["sq"], bc], [b_pp[0]])
                OP("act", lambda e: e.activation(out=A["nrm"][:], in_=pp[0][:], func=AF.Sqrt), [b_pp[0]], [bA["nrm"]])
                OP("dve", lambda e: e.tensor_scalar(out=A["nrm"][:], in0=A["nrm"][:], scalar1=1e-12, scalar2=None, op0=ALU.max), [bA["nrm"]], [bA["nrm"]])
                OP("dve", lambda e: e.reciprocal(out=A["nrm"][:], in_=A["nrm"][:]), [bA["nrm"]], [bA["nrm"]])
                OP("dve", lambda e: e.tensor_tensor(out=A["kk"][:], in0=A["kkr"][:], in1=A["nrm"][:], op=ALU.mult), [bA["kkr"], bA["nrm"]], [bA["kk"]])
                OP("dve", lambda e: e.tensor_scalar(out=A["kdm"][:], in0=A["a"][:], scalar1=pcol(pb + 11), scalar2=pcol(96 + i * 4 + 3), op0=ALU.mult, op1=ALU.add),
                   [bA["a"], b_par], [bA["kdm"]])
                OP("pool", lambda e: e.tensor_tensor(out=A["kd"][:], in0=sh["k"][:], in1=A["kdm"][:], op=ALU.mult), [b_sh["k"], bA["kdm"]], [bA["kd"]])
                OP("pool", lambda e: e.tensor_tensor(out=A["b"][:], in0=A["a"][:], in1=A["kk"][:], op=ALU.mult), [bA["a"], bA["kk"]], [bA["b"]])
                OP("dve", lambda e: e.tensor_tensor_scan(out=A["F"][:], data0=cst_sb[:, CI_CM:CI_CM + 4, :].rearrange("p a b -> p (a b)"), data1=A["lg"][:], initial=0.0,
                                                         op0=ALU.mult, op1=ALU.add), [bA["lg"], bc], [bA["F"]])
                F3 = A["F"][:].rearrange("p (c t) -> p c t", t=64)
                tot_bc = F3[:, :, 63:64].to_broadcast([128, NCH, 64])
                v3 = lambda n: A[n][:].rearrange("p (c t) -> p c t", t=64)
                if d == 0:
                    OP("dve", lambda e: e.tensor_tensor(out=A["E"][:], in0=A["F"][:], in1=A["lg"][:], op=ALU.subtract), [bA["F"], bA["lg"]], [bA["E"]])
                    Fi, bFi = A["F"], bA["F"]
                else:
                    OP("dve", lambda e: e.tensor_tensor(out=v3("E"), in0=tot_bc, in1=F3, op=ALU.subtract), [bA["F"]], [bA["E"]])
                    OP("dve", lambda e: e.tensor_tensor(out=A["Fi"][:], in0=A["E"][:], in1=A["lg"][:], op=ALU.add), [bA["E"], bA["lg"]], [bA["Fi"]])
                    Fi, bFi = A["Fi"], bA["Fi"]
                OP("act", lambda e, Fi=Fi: e.activation(out=A["eFi"][:], in_=Fi[:], func=AF.Exp), [bFi], [bA["eFi"]])
                OP("act", lambda e: e.activation(out=A["eE"][:], in_=A["E"][:], func=AF.Exp), [bA["E"]], [bA["eE"]])
                OP("act", lambda e, Fi=Fi: e.activation(out=A["enFi"][:], in_=Fi[:], func=AF.Exp, scale=-1.0), [bFi], [bA["enFi"]])
                OP("act", lambda e: e.activation(out=PC[:].unsqueeze(2), in_=F3[:, :, 63:64], func=AF.Exp), [bA["F"]], [b_PC])
                OP("dve", lambda e: e.tensor_tensor(out=A["rho"][:], in0=sh["r"][:], in1=A["eFi"][:], op=ALU.mult), [b_sh["r"], bA["eFi"]], [bA["rho"]])
                OP("pool", lambda e: e.tensor_tensor(out=A["al"][:], in0=A["kk"][:], in1=A["eE"][:], op=ALU.mult), [bA["kk"], bA["eE"]], [bA["al"]])
                OP("dve", lambda e: e.tensor_tensor(out=A["be"][:], in0=A["b"][:], in1=A["enFi"][:], op=ALU.mult), [bA["b"], bA["enFi"]], [bA["be"]])
                OP("pool", lambda e: e.tensor_tensor(out=A["ka"][:], in0=A["kd"][:], in1=A["enFi"][:], op=ALU.mult), [bA["kd"], bA["enFi"]], [bA["ka"]])
                for n_ in ("rho", "al", "be", "ka"):
                    OP("act", lambda e, n_=n_: e.copy(out=AB[n_][:], in_=A[n_][:]), [bA[n_]], [bAB[n_]])
                OP("dve", lambda e: e.scalar_tensor_tensor(out=A["rkd"][:], in0=sh["r"][:], scalar=pcol(pb + 12), in1=A["kd"][:], op0=ALU.mult, op1=ALU.mult),
                   [b_sh["r"], bA["kd"], b_par], [bA["rkd"]])
                OP("pe", lambda e: e.matmul(pp[1][:], lhsT=cst(CI_ONES), rhs=A["rkd"][:], start=True, stop=True), [bA["rkd"], bc], [b_pp[1]])
                if d == dirs[0]:
                    OP("dve", lambda e, t0=t0: e.tensor_tensor(out=bv[:, t0:t0 + SEG], in0=pp[1][:], in1=sh["v"][:], op=ALU.mult), [b_pp[1], b_sh["v"]], [b_bv])
                else:
                    OP("dve", lambda e: e.tensor_tensor(out=A["tmp"][:], in0=pp[1][:], in1=sh["v"][:], op=ALU.mult), [b_pp[1], b_sh["v"]], [bA["tmp"]])
                    OP("pool", lambda e, t0=t0: e.tensor_tensor(out=bv[:, t0:t0 + SEG], in0=bv[:, t0:t0 + SEG], in1=A["tmp"][:], op=ALU.add), [bA["tmp"], b_bv], [b_bv])
            def packs_phase(s):
                nonlocal gctr
                t0 = s * SEG
                packs = list(range(SEG // 128)) if d == 0 else list(range(SEG // 128 - 1, -1, -1))
                mA, mB = (8, 9) if d == 0 else (9, 8)
                mS, mI = (CI_MUS, CI_MUI) if d == 0 else (CI_MLS, CI_MLI)
                for hb in range(2):
                    hs = slice(hb * 64, hb * 64 + 64)
                    rd_in = [bA["rho"], bA["al"], bA["be"], bA["ka"]]

                    def fm(n, p):
                        return A[n][hs, p * 128:(p + 1) * 128]

                    def fb(n, p):
                        return AB[n][hs, p * 128:(p + 1) * 128]
                    rd_b = [bAB["rho"], bAB["al"], bAB["be"], bAB["ka"]]

                    def pad(tl, off, rows=slice(0, 128)):
                        return tl[rows, off:off + 128] if hb == 0 else tl[rows, off - 64:off + 64]

                    steps = []
                    def s1(l, p):
                        OP("pe", lambda e: e.matmul(bank[l][:, 0:128], lhsT=fb("be", p), rhs=fb("al", p), start=True, stop=True), rd_b, [b_bank[l]])
                        OP("pe", lambda e: e.matmul(bank[l][:, 128:256], lhsT=fb("al", p), rhs=fb("be", p), start=True, stop=True), rd_b, [b_bank[l]])
                        OP("pe", lambda e: e.matmul(bank[l][:, 256:384], lhsT=fb("ka", p), rhs=fb("al", p), start=True, stop=True), rd_b, [b_bank[l]])
                        OP("pe", lambda e: e.matmul(bank[l][:, 384:512], lhsT=fb("be", p), rhs=fb("rho", p), start=True, stop=True), rd_b, [b_bank[l]])
                        OP("dve", lambda e: e.tensor_tensor(out=NN[l][0][:].rearrange("p (a b) -> p a b", a=2), in0=bank[l][:, 0:256].rearrange("p (a b) -> p a b", a=2),
                                                            in1=cst_sb[:, 8:10, :] if d == 0 else cst_sb[:, 10:12, :], op=ALU.mult), [b_bank[l], bc], [b_NN[l][0]])
                        OP("dve", lambda e: e.tensor_tensor(out=MM[l][:, 0:256].rearrange("p (a b) -> p a b", a=2), in0=bank[l][:, 256:512].rearrange("p (a b) -> p a b", a=2),
                                                            in1=cst_sb[:, 12:14, :] if d == 0 else cst_sb[:, 14:16, :], op=ALU.mult), [b_bank[l], bc], [b_MM[l]])
                    steps.append(s1)
                    def s2(l, p):
                        S2V = 0
                        OP("pe", lambda e: e.matmul(alt[l][:, 0:128], lhsT=fb("ka", p), rhs=fb("rho", p), start=True, stop=True), rd_b, [b_alt[l]])
                        if S2V == 1:
                            OP("dve", lambda e: e.tensor_tensor(out=MM[l][:, 256:384], in0=alt[l][:, 0:128], in1=cst(mI), op=ALU.mult), [b_alt[l], bc], [b_MM[l]])
                            OP("dve", lambda e: e.tensor_tensor(out=PT[l][:], in0=NN[l][0][:, 0:128], in1=cst(CI_ID), op=ALU.add), [b_NN[l][0], bc], [b_PT[l]])
                            return
                        if S2V == 2:
                            for k_, src in enumerate((sh["v"], A["be"], A["ka"], None, A["al"])):
                                if src is None:
                                    continue
                                OP("pe", lambda e, k_=k_, src=src: e.transpose(out=alt[l][:, 128 + k_ * 64:128 + (k_ + 1) * 64], in_=src[hs, p * 128:(p + 1) * 128],
                                                                             identity=cst_sb[hs, CI_ID, hb * 64:hb * 64 + 64]),
                                   [b_sh["v"], bA["be"], bA["ka"], bA["al"], bc], [b_alt[l]])
                            OP("dve", lambda e: e.tensor_tensor(out=MM[l][:, 256:384], in0=alt[l][:, 0:128], in1=cst(mI), op=ALU.mult), [b_alt[l], bc], [b_MM[l]])
                            return
                        if S2V == 3:
                            OP("dve", lambda e: e.tensor_tensor(out=PT[l][:], in0=NN[l][0][:, 0:128], in1=cst(CI_ID), op=ALU.add), [b_NN[l][0], bc], [b_PT[l]])
                            return
                        if S2V == 4:
                            OP("act", lambda e: e.copy(out=tok5[l][:, AL0:AL0 + 64], in_=alt[l][:, 0:64]), [b_alt[l]], [b_tok5[l]])
                            return
                        if S2V == 5:
                            OP("act", lambda e: e.copy(out=tok5[l][:, 64:448].rearrange("p (a b) -> p a b", b=128)[:, :, 0:64],
                                                       in_=alt[l][:, 128:320].rearrange("p (a b) -> p a b", b=64)), [b_alt[l]], [b_tok5[l]])
                            return
                        for k_, src in enumerate((sh["v"], A["be"], A["ka"], None, A["al"])):
                            if src is None:
                                continue
                            OP("pe", lambda e, k_=k_, src=src: e.transpose(out=alt[l][:, 128 + k_ * 64:128 + (k_ + 1) * 64], in_=src[hs, p * 128:(p + 1) * 128],
                                                                         identity=cst_sb[hs, CI_ID, hb * 64:hb * 64 + 64]),
                               [b_sh["v"], bA["be"], bA["ka"], bA["al"], bc], [b_alt[l]])
                        OP("dve", lambda e: e.tensor_tensor(out=MM[l][:, 256:384], in0=alt[l][:, 0:128], in1=cst(mI), op=ALU.mult), [b_alt[l], bc], [b_MM[l]])
                        OP("dve", lambda e: e.tensor_copy(out=tok5[l][:, 64:448].rearrange("p (a b) -> p a b", b=128)[:, :, 0:64],
                                                   in_=alt[l][:, 128:320].rearrange("p (a b) -> p a b", b=64)), [b_alt[l]], [b_tok5[l]])
                        OP("dve", lambda e: e.tensor_copy(out=tok5[l][:, AL0:AL0 + 64], in_=alt[l][:, 384:448]), [b_alt[l]], [b_tok5[l]])
                        OP("dve", lambda e: e.tensor_tensor(out=PT[l][:], in0=NN[l][0][:, 0:128], in1=cst(CI_ID), op=ALU.add), [b_NN[l][0], bc], [b_PT[l]])
                    steps.append(s2)
                    for lev in range(5):
                        def s3(l, p, lev=lev):
                            cur, nxt = NN[l][lev % 2], NN[l][(lev + 1) % 2]
                            bcur, bnxt = b_NN[l][lev % 2], b_NN[l][(lev + 1) % 2]
                            OP("pe", lambda e: e.matmul(bank[l][:, 128:256], lhsT=cur[:, 0:128], rhs=cur[:, 128:256], start=True, stop=True), [bcur], [b_bank[l]])
                            OP("pe", lambda e: e.matmul(bank[l][:, 0:128], lhsT=cur[:, 128:256], rhs=cur[:, 0:128], start=True, stop=True), [bcur], [b_bank[l]])
                            OP("act", lambda e: e.copy(out=nxt[:, 0:256], in_=bank[l][:, 0:256]), [b_bank[l]], [bnxt])
                        def s4(l, p, lev=lev):
                            nxt, bnxt = NN[l][(lev + 1) % 2], b_NN[l][(lev + 1) % 2]
                            OP("pe", lambda e: e.matmul(alt[l][:, 256:384], lhsT=nxt[:, 128:256], rhs=PT[l][:], start=True, stop=True), [bnxt, b_PT[l]], [b_alt[l]])
                            OP("dve", lambda e: e.tensor_tensor(out=PT[l][:], in0=alt[l][:, 256:384], in1=PT[l][:], op=ALU.add), [b_alt[l], b_PT[l]], [b_PT[l]])
                        steps.append(s3)
                        steps.append(s4)
                    def s5(l, p):
                        OP("pe", lambda e: e.matmul(bank[l][:, 0:64], lhsT=MM[l][:, 0:128], rhs=tok5[l][:, V0:V0 + 64], start=True, stop=True), [b_MM[l], b_tok5[l]], [b_bank[l]])
                        OP("act", lambda e: e.copy(out=tok5[l][:, X10:X10 + 64], in_=bank[l][:, 0:64]), [b_bank[l]], [b_tok5[l]])
                    steps.append(s5)
                    def s6(l, p):
                        OP("pe", lambda e: e.matmul(bank[l][:, 64:192], lhsT=PT[l][:], rhs=tok5[l][:, X10:X10 + 128], start=True, stop=True), [b_PT[l], b_tok5[l]], [b_bank[l]])
                        OP("act", lambda e: e.activation(out=nW[l][:, 64:320].rearrange("p (a b) -> p a b", b=128)[:, :, 0:64], in_=bank[l][:, 64:192].rearrange("p (a b) -> p a b", b=64), func=AF.Copy, scale=-1.0), [b_bank[l]], [b_nW[l]])
                    steps.append(s6)
                    def s7(l, p):
                        tk = p * 128
                        OP("pe", lambda e: e.matmul(bank[l][:, 192:320], lhsT=pad(tok5[l], V0), rhs=MM[l][:, 256:384], start=True, stop=False), [b_tok5[l], b_MM[l]], [b_bank[l]])
                        OP("pe", lambda e: e.matmul(bank[l][:, 192:320], lhsT=pad(nW[l], NW1), rhs=MM[l][:, 128:256], start=False, stop=True), [b_nW[l], b_MM[l]], [b_bank[l]])
                        OP("pe", lambda e: e.matmul(bank[l][:, 320:448], lhsT=pad(nW[l], NW2), rhs=MM[l][:, 128:256], start=True, stop=True), [b_nW[l], b_MM[l]], [b_bank[l]])
                        if d == dirs[0]:
                            OP("dve", lambda e: e.tensor_copy(out=yacc[hs, t0 + tk:t0 + tk + 128], in_=bank[l][hs, 192:320]), [b_bank[l]], [b_yacc])
                        else:
                            OP("dve", lambda e: e.tensor_tensor(out=yacc[hs, t0 + tk:t0 + tk + 128], in0=bank[l][hs, 192:320], in1=yacc[hs, t0 + tk:t0 + tk + 128], op=ALU.add),
                               [b_bank[l], b_yacc], [b_yacc])
                        OP("dve", lambda e: e.tensor_tensor(out=QtT[l][hs, :], in0=bank[l][hs, 320:448], in1=fm("rho", p), op=ALU.add), [b_bank[l], bA["rho"]], [b_QtT[l]])
                    steps.append(s7)
                    def s8(l, p):
                        for cc in range(2):
                            cs = slice(cc * 64, cc * 64 + 64)
                            tgt, btgt = (bank[l], b_bank[l]) if cc == 0 else (alt[l], b_alt[l])
                            ch = p * 2 + cc
                            OP("pe", lambda e, cs=cs, tgt=tgt: e.matmul(tgt[:, 0:64], lhsT=pad(nW[l], NW2, cs), rhs=tok5[l][cs, BE0:BE0 + 64], start=True, stop=True),
                               [b_nW[l], b_tok5[l]], [btgt])
                            OP("pe", lambda e, cs=cs, tgt=tgt: e.matmul(tgt[:, 64:128], lhsT=pad(tok5[l], KA0, cs), rhs=tok5[l][cs, V0:V0 + 64], start=True, stop=False),
                               [b_tok5[l]], [btgt])
                            OP("pe", lambda e, cs=cs, tgt=tgt: e.matmul(tgt[:, 64:128], lhsT=pad(tok5[l], BE0, cs), rhs=nW[l][cs, NW1:NW1 + 64], start=False, stop=True),
                               [b_tok5[l], b_nW[l]], [btgt])
                            OP("dve", lambda e, cc=cc, tgt=tgt: e.tensor_tensor(out=GpT[l][hs, cc * 64:cc * 64 + 64], in0=tgt[hs, 0:64],
                                                                in1=cst_sb[hs, CI_ID, hb * 64:hb * 64 + 64], op=ALU.add),
                               [btgt, bc], [b_GpT[l]])
                            OP("dve", lambda e, cc=cc, tgt=tgt, ch=ch: e.tensor_scalar(out=Hs[l][hs, cc * 64:cc * 64 + 64], in0=tgt[hs, 64:128], scalar1=PC[hs, ch:ch + 1], scalar2=None, op0=ALU.mult),
                               [btgt, b_PC], [b_Hs[l]])
                    steps.append(s8)
                    STOP = 99
                    for g0 in range(0, len(packs), LANES):
                        grp = packs[g0:g0 + LANES]
                        gp = gctr % 2
                        gctr += 1
                        QtT, GpT, Hs = QtTs[gp], GpTs[gp], Hss[gp]
                        b_QtT, b_GpT, b_Hs = b_QtTs[gp], b_GpTs[gp], b_Hss[gp]
                        for si_, stp in enumerate(steps):
                            for l, p in enumerate(grp):
                                stp(l, p)
                            if pend and si_ % 3 == 2:
                                pend.pop(0)()
                            for _ in range(2):
                                if dprep:
                                    dprep.pop(0)()
                        while pend:
                            pend.pop(0)()
                        for l, p in enumerate(grp):
                            for cc in ((0, 1) if d == 0 else (1, 0)):
                                def chain_step(l=l, p=p, cc=cc, hs=hs, QtT=QtT, GpT=GpT, Hs=Hs, b_QtT=b_QtT, b_GpT=b_GpT, b_Hs=b_Hs, PC=PC, b_PC=b_PC, t0=t0):
                                    cs = slice(cc * 64, cc * 64 + 64)
                                    k2 = nchain[0] % 2
                                    nchain[0] += 1
                                    tk = t0 + p * 128 + cc * 64
                                    ch = p * 2 + cc
                                    OP("pe", lambda e: e.matmul(pch[k2][hs, 0:64], lhsT=ST[hs, :], rhs=QtT[l][hs, cs], start=True, stop=True), [b_ST, b_QtT[l]], [b_pch[k2]])
                                    OP("pe", lambda e: e.matmul(pch[k2][hs, 64:128], lhsT=GpT[l][hs, cs], rhs=ST[hs, :], start=True, stop=True), [b_ST, b_GpT[l]], [b_pch[k2]])
                                    OP("dve", lambda e: e.tensor_tensor(out=yacc[hs, tk:tk + 64], in0=pch[k2][hs, 0:64], in1=yacc[hs, tk:tk + 64], op=ALU.add), [b_pch[k2], b_yacc], [b_yacc])
                                    OP("dve", lambda e: e.scalar_tensor_tensor(out=ST[hs, :], in0=pch[k2][hs, 64:128], scalar=PC[hs, ch:ch + 1], in1=Hs[l][hs, cs], op0=ALU.mult, op1=ALU.add),
                                       [b_pch[k2], b_PC, b_Hs[l]], [b_ST])
                                pend.append(chain_step)
                while pend:
                    pend.pop(0)()
                while dprep:
                    dprep.pop(0)()

            dprep = []
            for k_seg, s in enumerate(segs):
                par_k = (segbase[0] + k_seg) % 2
                if k_seg == 0:
                    bind_par(par_k)
                    prep(s)
                if k_seg + 1 < len(segs):
                    bind_par(1 - par_k)
                    DEFER[0] = dprep
                    prep(segs[k_seg + 1])
                    DEFER[0] = None
                bind_par(par_k)
                packs_phase(s)
            segbase[0] += len(segs)
        for s in range(nseg):
            t0 = s * SEG
            ys = yacc[:, t0:t0 + SEG]
            OP("pe", lambda e, ys=ys: e.matmul(pp[0][:], lhsT=cst(CI_ONES), rhs=ys, start=True, stop=True), [b_yacc, bc], [b_pp[0]])
            OP("dve", lambda e, ys=ys: e.scalar_tensor_tensor(out=A["tmp"][:], in0=pp[0][:], scalar=-1.0 / 64, in1=ys, op0=ALU.mult, op1=ALU.add), [b_pp[0], b_yacc], [bA["tmp"]])
            OP("pool", lambda e: e.tensor_tensor(out=A["sq"][:], in0=A["tmp"][:], in1=A["tmp"][:], op=ALU.mult), [bA["tmp"]], [bA["sq"]])
            OP("pe", lambda e: e.matmul(pp[1][:], lhsT=cst(CI_ONES), rhs=A["sq"][:], start=True, stop=True), [bA["sq"], bc], [b_pp[1]])
            OP("act", lambda e: e.activation(out=A["nrm"][:], in_=pp[1][:], func=AF.Sqrt, scale=1.0 / 64, bias=par_sb[:, 81:82]), [b_pp[1], b_par], [bA["nrm"]])
            OP("dve", lambda e: e.reciprocal(out=A["nrm"][:], in_=A["nrm"][:]), [bA["nrm"]], [bA["nrm"]])
            OP("dve", lambda e: e.tensor_tensor(out=A["tmp"][:], in0=A["tmp"][:], in1=A["nrm"][:], op=ALU.mult), [bA["tmp"], bA["nrm"]], [bA["tmp"]])
            OP("dve", lambda e: e.tensor_scalar(out=A["tmp"][:], in0=A["tmp"][:], scalar1=pcol(pb + 13), scalar2=pcol(pb + 14), op0=ALU.mult, op1=ALU.add), [bA["tmp"], b_par], [bA["tmp"]])
            OP("pool", lambda e, t0=t0: e.tensor_tensor(out=A["rkd"][:], in0=A["tmp"][:], in1=bv[:, t0:t0 + SEG], op=ALU.add), [bA["tmp"], b_bv], [bA["rkd"]])
            P.dma("sp", lambda e, t0=t0, i=i: e.dma_start(out=yT_d[i * 128:(i + 1) * 128, t0:t0 + SEG], in_=A["rkd"][:]), bA["rkd"], reads=[bA["rkd"]], writes=[(b_yT or c.b_yT)[i]])


CI_TGE, CI_TLE = 20, 21
TWO_PI = 6.283185307179586
C1_2PI = 6.28125
C2_2PI = TWO_PI - 6.28125
PI_LO = 3.1415925


def emit_attn(P, c, nc, st_outer, projT_d, yT_d, pos_d, mix_d, par_sb, b_par, cst_sb, scratch, tt=T, tb=0, b_yT=None, mrow=0, final=True, sfx=""):
    NB = tt // 128
    stA = ExitStack()
    stB = ExitStack()
    stC = ExitStack()
    cur = [st_outer]
    sb = lambda name, shape, dt: cur[0].enter_context(nc.sbuf_tensor(name + sfx, shape, dt))
    ps = lambda name, shape, dt: cur[0].enter_context(nc.psum_tensor(name + sfx, shape, dt))
    f = lambda name, cols, dt=F32: sb(name, [128, cols], dt)
    B = lambda n: P.buf(n)
    qr = sb("qr", [128, 4, tt], BF16)
    kr = sb("kr", [128, 2, tt], BF16)
    Vd = sb("Vd", [128, NB, 2, 128], BF16)
    ones_bf = sb("ones_bf", [128, 128], BF16)
    es = f("es", 8)
    cur[0] = stB
    cosT = f("cosT", tt); sinT = f("sinT", tt)
    cur[0] = stA
    bc = c.b_const

    def OP(eng, fn, reads, writes):
        P.op(eng, fn, reads=reads, writes=writes)

    posi = sb("posi", [128, tt], I32)
    ang = f("ang", tt); nn_ = f("nn_", tt); rr = f("rr", tt)
    msk = nn_
    ni = posi
    b_posi, b_ang, b_nn, b_rr, b_cos, b_sin = [B(n) for n in "posi ang nn rr cos sin".split()]
    b_msk, b_ni = b_nn, b_posi
    P.dma("sp", lambda e: e.dma_start(out=posi[:], in_=pos_d[0:1, :].partition_broadcast(128)), b_posi, writes=[b_posi])
    OP("dve", lambda e: e.tensor_copy(out=ang[:], in_=posi[:]), [b_posi], [b_ang])
    OP("dve", lambda e: e.tensor_scalar(out=ang[:], in0=ang[:], scalar1=par_sb[:, 82:83], scalar2=None, op0=ALU.mult), [b_ang, b_par], [b_ang])

    def reduce_to(dst, b_dst, src, b_src):
        OP("dve", lambda e: e.tensor_scalar(out=nn_[:], in0=src[:], scalar1=1.0 / TWO_PI, scalar2=None, op0=ALU.mult), [b_src], [b_nn])
        OP("dve", lambda e: e.tensor_copy(out=ni[:], in_=nn_[:]), [b_nn], [b_ni])
        OP("dve", lambda e: e.tensor_copy(out=nn_[:], in_=ni[:]), [b_ni], [b_nn])
        OP("dve", lambda e: e.scalar_tensor_tensor(out=dst[:], in0=nn_[:], scalar=-C1_2PI, in1=src[:], op0=ALU.mult, op1=ALU.add), [b_nn, b_src], [b_dst])
        OP("dve", lambda e: e.scalar_tensor_tensor(out=dst[:], in0=nn_[:], scalar=-C2_2PI, in1=dst[:], op0=ALU.mult, op1=ALU.add), [b_nn, b_dst], [b_dst])
        for _ in range(2):
            OP("dve", lambda e: e.tensor_scalar(out=msk[:], in0=dst[:], scalar1=3.14159265, scalar2=None, op0=ALU.is_gt), [b_dst], [b_msk])
            OP("dve", lambda e: e.scalar_tensor_tensor(out=dst[:], in0=msk[:], scalar=-TWO_PI, in1=dst[:], op0=ALU.mult, op1=ALU.add), [b_msk, b_dst], [b_dst])
            OP("dve", lambda e: e.tensor_scalar(out=msk[:], in0=dst[:], scalar1=-3.14159265, scalar2=None, op0=ALU.is_lt), [b_dst], [b_msk])
            OP("dve", lambda e: e.scalar_tensor_tensor(out=dst[:], in0=msk[:], scalar=TWO_PI, in1=dst[:], op0=ALU.mult, op1=ALU.add), [b_msk, b_dst], [b_dst])
        OP("dve", lambda e: e.tensor_scalar(out=dst[:], in0=dst[:], scalar1=PI_LO, scalar2=-PI_LO, op0=ALU.min, op1=ALU.max), [b_dst], [b_dst])

    reduce_to(rr, b_rr, ang, b_ang)
    OP("act", lambda e: e.activation(out=sinT[:], in_=rr[:], func=AF.Sin), [b_rr], [b_sin])
    OP("dve", lambda e: e.tensor_scalar(out=ang[:], in0=rr[:], scalar1=1.5707963267948966, scalar2=None, op0=ALU.add), [b_rr], [b_ang])
    reduce_to(rr, b_rr, ang, b_ang)
    OP("act", lambda e: e.activation(out=cosT[:], in_=rr[:], func=AF.Sin), [b_rr], [b_cos])

    stA.close()
    P.barrier(lambda e: e.memset(scratch, 0.0))
    cur[0] = stB
    rawq = [f(f"rawq{k}", tt) for k in range(2)]
    t1 = [f(f"rp_t1_{k}", 512) for k in range(2)]
    t2 = [f(f"rp_t2_{k}", 512) for k in range(2)]
    pp = [ps(f"app{k}", [128, 512], F32) for k in range(2)]
    b_qr = [B(f"qr{k}") for k in range(4)]
    b_kr = [B(f"kr{k}") for k in range(2)]
    b_Vd, b_ones, b_es = B("Vd"), B("ones_bf"), B("es")
    b_rawq = [B(f"rawq{k}") for k in range(2)]
    b_t1 = [B(f"rp_t1_{k}") for k in range(2)]
    b_t2 = [B(f"rp_t2_{k}") for k in range(2)]
    b_pp = [B(f"app{k}") for k in range(2)]
    OP("dve", lambda e: e.memset(ones_bf[:], 1.0), [], [b_ones])
    OP("act", lambda e: e.activation(out=es[:], in_=par_sb[:, 84:92], func=AF.Exp), [b_par], [b_es])
    cnt = 0
    for kind, k_, trow in [("q", 0, tb + T_Q), ("q", 1, tb + T_Q + 1), ("q", 2, tb + T_Q + 2), ("q", 3, tb + T_Q + 3), ("k", 0, tb + T_KA), ("k", 1, tb + T_KB)]:
        ri = cnt % 2
        cnt += 1
        P.dma("sp", lambda e: e.dma_start(out=rawq[ri][:], in_=projT_d[trow * 128:(trow + 1) * 128, :]), b_rawq[ri], reads=[c.b_projT[trow]], writes=[b_rawq[ri]])
        dst = qr[:, k_, :] if kind == "q" else kr[:, k_, :]
        bdst = b_qr[k_] if kind == "q" else b_kr[k_]
        for tg in range(tt // 512):
            pi = tg % 2
            sl = slice(tg * 512, (tg + 1) * 512)
            OP("pe", lambda e: e.matmul(pp[pi][:], lhsT=cst_sb[:, CI_ROT, :], rhs=rawq[ri][:, sl], start=True, stop=True), [b_rawq[ri], bc], [b_pp[pi]])
            OP("dve", lambda e: e.tensor_tensor(out=t2[pi][:], in0=pp[pi][:], in1=sinT[:, sl], op=ALU.mult), [b_pp[pi], b_sin], [b_t2[pi]])
            OP("pool", lambda e: e.tensor_tensor(out=t1[pi][:], in0=rawq[ri][:, sl], in1=cosT[:, sl], op=ALU.mult), [b_rawq[ri], b_cos], [b_t1[pi]])
            OP("dve", lambda e: e.tensor_tensor(out=dst[:, sl], in0=t1[pi][:], in1=t2[pi][:], op=ALU.add), [b_t1[pi], b_t2[pi]], [bdst])
    ri = cnt % 2
    P.dma("sp", lambda e: e.dma_start(out=rawq[ri][:], in_=projT_d[(tb + T_VAT) * 128:(tb + T_VAT + 1) * 128, :]), b_rawq[ri], reads=[c.b_projT[tb + T_VAT]], writes=[b_rawq[ri]])
    for n in range(NB):
        pi = n % 2
        OP("pe", lambda e: e.transpose(out=pp[pi][:, 0:128], in_=rawq[ri][:, n * 128:(n + 1) * 128], identity=cst_sb[:, CI_ID, :]), [b_rawq[ri], bc], [b_pp[pi]])
        for g in range(2):
            OP("dve", lambda e: e.tensor_copy(out=Vd[:, n, g, :].rearrange("p (a b) -> p a b", a=2), in_=pp[pi][:, g * 64:(g + 1) * 64].unsqueeze(1).to_broadcast([128, 2, 64])),
               [b_pp[pi]], [b_Vd])

    stB.close()
    P.barrier(lambda e: e.memset(scratch, 0.0))
    cur[0] = stC
    sA = [ps(f"sA{k}", [128, 512], F32) for k in range(2)]
    sB = [ps(f"sB{k}", [128, 512], F32) for k in range(2)]
    po = ps("po", [128, 512], F32)
    pd = ps("pd", [128, 512], F32)
    Pt = [sb(f"Pt{k}", [128, 512], BF16) for k in range(3)]
    den = f("den", 512); onr = f("onr", 512)
    ya = [f(f"ya{k}", tt) for k in range(2)]
    gt = [f(f"gt{k}", tt) for k in range(2)]
    c.stC = stC
    b_sA = [B(f"sA{k}") for k in range(2)]
    b_sB = [B(f"sB{k}") for k in range(2)]
    b_po, b_pd, b_den, b_onr = B("po"), B("pd"), B("den"), B("onr")
    b_Pt = [B(f"Pt{k}") for k in range(3)]
    b_ya = [B(f"ya{k}") for k in range(2)]
    b_gt = [B(f"gt{k}") for k in range(2)]
    nsc = 0
    for g in range(2):
        for n in range(NB):
            kbs = [kb for kb in (n - 1, n, n + 1) if 0 <= kb < NB]
            qs = slice(n * 128, (n + 1) * 128)
            for idx, kb in enumerate(kbs):
                k2 = nsc % 2
                nsc += 1
                ks = slice(kb * 128, (kb + 1) * 128)
                OP("pe", lambda e: e.matmul(sA[k2][:, 0:256], lhsT=kr[0:64, g, ks], rhs=qr[0:64, 2 * g:2 * g + 2, qs], start=True, stop=True),
                   [b_kr[g], b_qr[2 * g], b_qr[2 * g + 1]], [b_sA[k2]])
                OP("pe", lambda e: e.matmul(sB[k2][:, 0:256], lhsT=kr[64:128, g, ks], rhs=qr[64:128, 2 * g:2 * g + 2, qs], start=True, stop=True),
                   [b_kr[g], b_qr[2 * g], b_qr[2 * g + 1]], [b_sB[k2]])
                OP("act", lambda e: e.activation(out=Pt[idx][:, 0:256], in_=sA[k2][:, 0:256], func=AF.Exp, scale=0.125), [b_sA[k2]], [b_Pt[idx]])
                OP("act", lambda e: e.activation(out=Pt[idx][:, 256:512], in_=sB[k2][:, 0:256], func=AF.Exp, scale=0.125), [b_sB[k2]], [b_Pt[idx]])
                if kb != n:
                    mi = CI_TGE if kb < n else CI_TLE
                    OP("dve", lambda e: e.tensor_tensor(out=Pt[idx][:].rearrange("p (a b) -> p a b", a=4), in0=Pt[idx][:].rearrange("p (a b) -> p a b", a=4),
                                                        in1=cst_sb[:, mi, :].unsqueeze(1).to_broadcast([128, 4, 128]), op=ALU.mult), [b_Pt[idx], bc], [b_Pt[idx]])
            for idx, kb in enumerate(kbs):
                OP("pe", lambda e: e.matmul(po[:], lhsT=Vd[:, kb, g, :], rhs=Pt[idx][:], start=(idx == 0), stop=(idx == len(kbs) - 1)), [b_Vd, b_Pt[idx]], [b_po])
            for idx, kb in enumerate(kbs):
                OP("pe", lambda e: e.matmul(pd[:], lhsT=ones_bf[:], rhs=Pt[idx][:], start=(idx == 0), stop=(idx == len(kbs) - 1)), [b_ones, b_Pt[idx]], [b_pd])
            OP("dve", lambda e: e.tensor_tensor(out=den[:].rearrange("p (a b) -> p a b", a=4), in0=pd[:].rearrange("p (a b) -> p a b", a=4),
                                                in1=es[:, g * 4:(g + 1) * 4].unsqueeze(2).to_broadcast([128, 4, 128]), op=ALU.add), [b_pd, b_es], [b_den])
            OP("dve", lambda e: e.reciprocal(out=den[:], in_=den[:]), [b_den], [b_den])
            OP("dve", lambda e: e.tensor_tensor(out=onr[:], in0=po[:], in1=den[:], op=ALU.mult), [b_po, b_den], [b_onr])
            for j in range(2):
                OP("pool", lambda e: e.tensor_copy(out=ya[j][0:64, qs], in_=onr[0:64, j * 128:(j + 1) * 128]), [b_onr], [b_ya[j]])
                OP("pool", lambda e: e.tensor_copy(out=ya[j][64:128, qs], in_=onr[64:128, 256 + j * 128:256 + (j + 1) * 128]), [b_onr], [b_ya[j]])
        for j in range(2):
            at = 2 * g + j
            trow = tb + T_GAT + at
            P.dma("sp", lambda e: e.dma_start(out=gt[j][:], in_=projT_d[trow * 128:(trow + 1) * 128, :]), b_gt[j], reads=[c.b_projT[trow]], writes=[b_gt[j]])
            OP("act", lambda e: e.activation(out=gt[j][:], in_=gt[j][:], func=AF.Silu), [b_gt[j]], [b_gt[j]])
            OP("dve", lambda e: e.tensor_tensor(out=ya[j][:], in0=ya[j][:], in1=gt[j][:], op=ALU.mult), [b_ya[j], b_gt[j]], [b_ya[j]])
            P.dma("sp", lambda e: e.dma_start(out=mix_d[mrow + 512 + at * 128:mrow + 512 + (at + 1) * 128, :], in_=ya[j][:]), b_ya[j], reads=[b_ya[j]], writes=[c.b_mix], final=final)
    for i in range(4):
        j = i % 2
        trow = tb + T_GRW + i
        P.dma("sp", lambda e: e.dma_start(out=gt[j][:], in_=projT_d[trow * 128:(trow + 1) * 128, :]), b_gt[j], reads=[c.b_projT[trow]], writes=[b_gt[j]])
        P.dma("sp", lambda e: e.dma_start(out=ya[j][:], in_=yT_d[i * 128:(i + 1) * 128, :]), b_ya[j], reads=[(b_yT or c.b_yT)[i]], writes=[b_ya[j]])
        OP("act", lambda e: e.activation(out=gt[j][:], in_=gt[j][:], func=AF.Silu), [b_gt[j]], [b_gt[j]])
        OP("dve", lambda e: e.tensor_tensor(out=ya[j][:], in0=ya[j][:], in1=gt[j][:], op=ALU.mult), [b_ya[j], b_gt[j]], [b_ya[j]])
        P.dma("sp", lambda e: e.dma_start(out=mix_d[mrow + i * 128:mrow + (i + 1) * 128, :], in_=ya[j][:]), b_ya[j], reads=[b_ya[j]], writes=[c.b_mix], final=final)
    stC.close()


def build_B(final, ntok=2048):
    nc = bass.Bass("TRN2", target_bir_lowering=False)
    mixT_d = nc.dram_tensor("mixT", [D, ntok], F32, kind="ExternalInput").ap()
    x_d = nc.dram_tensor("x", [ntok, D], F32, kind="ExternalInput").ap()
    wo_d = nc.dram_tensor("wout", [D, D], F32, kind="ExternalInput").ap()
    fg_d = nc.dram_tensor("fg", [128, D], F32, kind="ExternalInput").ap()
    out_d = nc.dram_tensor("out", [ntok, D], F32, kind="ExternalOutput").ap()
    with ExitStack() as st:
        sb = lambda name, shape, dt: st.enter_context(nc.sbuf_tensor(name, shape, dt))
        ps = lambda name, shape, dt: st.enter_context(nc.psum_tensor(name, shape, dt))
        P = Prog(nc)
        B = lambda n: P.buf(n)
        mixb = sb("mixb", [128, KC, ntok], BF16)
        wob = sb("wob", [128, KC, D], BF16)
        fg = sb("fg_sb", [128, D], F32)
        xt = [sb(f"xt{i}", [128, D], F32) for i in range(2)]
        xo = [sb(f"xo{i}", [128, D], F32) for i in range(2)]
        junk = sb("junk", [128, D], BF16)
        stat = [sb(f"stat{i}", [128, 2], F32) for i in range(2)]
        epsc = sb("epsc", [128, 1], F32)
        pm = [ps(f"pm{i}", [128, 512], F32) for i in range(4)]
        b_mix = [B(f"mixb{k}") for k in range(KC)]
        b_wo = [B(f"wob{k}") for k in range(KC)]
        b_fg, b_junk, b_eps = B("fg"), B("junk"), B("eps")
        b_xt = [B(f"xt{i}") for i in range(2)]
        b_xo = [B(f"xo{i}") for i in range(2)]
        b_stat = [B(f"stat{i}") for i in range(2)]
        b_pm = [B(f"pm{i}") for i in range(4)]
        P.op("dve", lambda e: e.memset(epsc[:], 1e-6), writes=[b_eps])
        P.dma("sp", lambda e: e.dma_start(out=fg[:], in_=fg_d[:, :]), b_fg, writes=[b_fg])
        for kc in range(KC):
            P.dma("pool", lambda e: e.dma_start(out=mixb[:, kc, :], in_=mixT_d[kc * 128:(kc + 1) * 128, :]), b_mix[kc], writes=[b_mix[kc]])
            P.dma("pool", lambda e: e.dma_start(out=wob[:, kc, :], in_=wo_d[kc * 128:(kc + 1) * 128, :]), b_wo[kc], writes=[b_wo[kc]])
        nmm = 0
        for ti in range(ntok // 128):
            i = ti % 2
            ts = slice(ti * 128, (ti + 1) * 128)
            P.dma("sp", lambda e: e.dma_start(out=xt[i][:], in_=x_d[ts, :]), b_xt[i], writes=[b_xt[i]])
            for dg in range(D // 512):
                pi = nmm % 4
                nmm += 1
                ds_ = slice(dg * 512, (dg + 1) * 512)
                for kc in range(KC):
                    P.op("pe", lambda e: e.matmul(pm[pi][:], lhsT=mixb[:, kc, ts], rhs=wob[:, kc, ds_], start=(kc == 0), stop=(kc == KC - 1)),
                         reads=[b_mix[kc], b_wo[kc]], writes=[b_pm[pi]])
                P.op("dve", lambda e: e.tensor_tensor(out=xo[i][:, ds_], in0=pm[pi][:], in1=xt[i][:, ds_], op=ALU.add), reads=[b_pm[pi], b_xt[i]], writes=[b_xo[i]])
            if final:
                P.op("dve", lambda e: e.memset(stat[i][:], 0.0), writes=[b_stat[i]])
                P.op("act", lambda e: e.activation(out=junk[:], in_=xo[i][:], func=AF.Square, accum_out=stat[i][:, 0:1]), reads=[b_xo[i]], writes=[b_junk, b_stat[i]])
                P.op("act", lambda e: e.activation(out=stat[i][:, 1:2], in_=stat[i][:, 0:1], func=AF.Sqrt, scale=1.0 / D, bias=epsc[:]), reads=[b_stat[i], b_eps], writes=[b_stat[i]])
                P.op("dve", lambda e: e.reciprocal(out=stat[i][:, 1:2], in_=stat[i][:, 1:2]), reads=[b_stat[i]], writes=[b_stat[i]])
                P.op("dve", lambda e: e.scalar_tensor_tensor(out=xo[i][:], in0=xo[i][:], scalar=stat[i][:, 1:2], in1=fg[:], op0=ALU.mult, op1=ALU.mult),
                     reads=[b_xo[i], b_stat[i], b_fg], writes=[b_xo[i]])
            P.dma("sp", lambda e: e.dma_start(out=out_d[ts, :], in_=xo[i][:]), b_xo[i], reads=[b_xo[i]], final=True)
        P.emit(st)
        print("B", P.stats)
    return nc


def run_module(inputs, run_fn):
    x = np.ascontiguousarray(np.asarray(inputs["x"], np.float32))
    pos = np.asarray(inputs["positions"]).astype(np.int32)
    cst = make_consts()
    fg = np.ascontiguousarray(np.broadcast_to(np.asarray(inputs["final_g"], np.float32)[None, :], (128, D)))
    for l in range(2):
        ncA = build_A(stage=3, tt=T)
        in_maps = []
        for cid in range(8):
            b, hh = cid // 2, cid % 2
            in_maps.append({"x": np.ascontiguousarray(x[b]), "wsel": make_wsel(np.asarray(inputs["w_in"][l], np.float32), hh),
                            "par": make_par(inputs, l, hh), "cst": cst, "lora": make_lora(inputs, l, hh),
                            "pos": np.ascontiguousarray(pos[b:b + 1])})
        resA = run_fn(ncA, in_maps)
        ncB = build_B(final=(l == 1))
        in_maps = []
        wo = np.ascontiguousarray(np.asarray(inputs["w_out"][l], np.float32))
        for cid in range(8):
            b, th = cid // 2, cid % 2
            m0, m1 = resA[2 * b]["mix"], resA[2 * b + 1]["mix"]
            mixT = np.concatenate([m0[:512], m1[:512], m0[512:], m1[512:]], axis=0)
            in_maps.append({"mixT": np.ascontiguousarray(mixT[:, th * 2048:(th + 1) * 2048]),
                            "x": np.ascontiguousarray(x[b, th * 2048:(th + 1) * 2048]), "wout": wo, "fg": fg})
        resB = run_fn(ncB, in_maps)
        xn = np.empty_like(x)
        for cid in range(8):
            b, th = cid // 2, cid % 2
            xn[b, th * 2048:(th + 1) * 2048] = resB[cid]["out"]
        x = xn
    return x


NT2 = 2 * NT
NCOL2 = NT2 * 128


def emit_B(P, c, nc, sb, ps, mix_d, b_mixsrc, x_src, b_xsrc, wo_d, fg_sb, b_fg, epsc, b_eps, out_d, b_out, final, tt=T):
    B = lambda n: P.buf(n)
    HT = 2048
    mixb = sb("mixb", [128, KC, HT], BF16)
    wob = sb("wob", [128, KC, D], BF16)
    xt = [sb(f"bxt{i}", [128, D], F32) for i in range(2)]
    xo = [sb(f"bxo{i}", [128, D], F32) for i in range(2)]
    junk = sb("bjunk", [128, D], BF16)
    stat = [sb(f"bstat{i}", [128, 2], F32) for i in range(2)]
    pm = [ps(f"bpm{i}", [128, 512], F32) for i in range(4)]
    b_mix = [B(f"mixb{k}") for k in range(KC)]
    b_wo = [B(f"wob{k}") for k in range(KC)]
    b_junk = B("bjunk")
    b_xt = [B(f"bxt{i}") for i in range(2)]
    b_xo = [B(f"bxo{i}") for i in range(2)]
    b_stat = [B(f"bstat{i}") for i in range(2)]
    b_pm = [B(f"bpm{i}") for i in range(4)]
    for kc in range(KC):
        P.dma("pool", lambda e: e.dma_start(out=wob[:, kc, :], in_=wo_d[kc * 128:(kc + 1) * 128, :]), b_wo[kc], writes=[b_wo[kc]])
    nmm = 0
    for half in range(tt // HT):
        for kc in range(KC):
            P.dma("pool", lambda e: e.dma_start(out=mixb[:, kc, :], in_=mix_d[kc * 128:(kc + 1) * 128, half * HT:(half + 1) * HT]), b_mix[kc],
                  reads=[b_mixsrc], writes=[b_mix[kc]])
        for ti in range(HT // 128):
            i = ti % 2
            ts = slice(ti * 128, (ti + 1) * 128)
            gs = slice(half * HT + ti * 128, half * HT + (ti + 1) * 128)
            P.dma("sp", lambda e: e.dma_start(out=xt[i][:], in_=x_src[gs, :]), b_xt[i], reads=[b_xsrc], writes=[b_xt[i]])
            for dg in range(D // 512):
                pi = nmm % 4
                nmm += 1
                ds_ = slice(dg * 512, (dg + 1) * 512)
                for kc in range(KC):
                    P.op("pe", lambda e: e.matmul(pm[pi][:], lhsT=mixb[:, kc, ts], rhs=wob[:, kc, ds_], start=(kc == 0), stop=(kc == KC - 1)),
                         reads=[b_mix[kc], b_wo[kc]], writes=[b_pm[pi]])
                P.op("dve", lambda e: e.tensor_tensor(out=xo[i][:, ds_], in0=pm[pi][:], in1=xt[i][:, ds_], op=ALU.add), reads=[b_pm[pi], b_xt[i]], writes=[b_xo[i]])
            if final:
                P.op("dve", lambda e: e.memset(stat[i][:], 0.0), writes=[b_stat[i]])
                P.op("act", lambda e: e.activation(out=junk[:], in_=xo[i][:], func=AF.Square, accum_out=stat[i][:, 0:1]), reads=[b_xo[i]], writes=[b_junk, b_stat[i]])
                P.op("act", lambda e: e.activation(out=stat[i][:, 1:2], in_=stat[i][:, 0:1], func=AF.Sqrt, scale=1.0 / D, bias=epsc), reads=[b_stat[i], b_eps], writes=[b_stat[i]])
                P.op("dve", lambda e: e.reciprocal(out=stat[i][:, 1:2], in_=stat[i][:, 1:2]), reads=[b_stat[i]], writes=[b_stat[i]])
                P.op("dve", lambda e: e.scalar_tensor_tensor(out=xo[i][:], in0=xo[i][:], scalar=stat[i][:, 1:2], in1=fg_sb, op0=ALU.mult, op1=ALU.mult),
                     reads=[b_xo[i], b_stat[i], b_fg], writes=[b_xo[i]])
            P.dma("sp", lambda e: e.dma_start(out=out_d[gs, :], in_=xo[i][:]), b_xo[i], reads=[b_xo[i]], writes=[b_out], final=final)


def build_fused(tt=T):
    nc = bass.Bass("TRN2", target_bir_lowering=False)
    c = Ctx()
    x_d = nc.dram_tensor("x", [tt, D], F32, kind="ExternalInput").ap()
    wsel_d = [nc.dram_tensor(f"wsel{l}", [D, NCOL2], F32, kind="ExternalInput").ap() for l in range(2)]
    wo_d = [nc.dram_tensor(f"wout{l}", [D, D], F32, kind="ExternalInput").ap() for l in range(2)]
    par_d = nc.dram_tensor("par", [128, 4, NPAR], F32, kind="ExternalInput").ap()
    cst_d = nc.dram_tensor("cst", [128, NCST, 128], F32, kind="ExternalInput").ap()
    lora_d = nc.dram_tensor("lora", [4, 128, 4, 512], F32, kind="ExternalInput").ap()
    pos_d = nc.dram_tensor("pos", [1, tt], I32, kind="ExternalInput").ap()
    fg_d = nc.dram_tensor("fg", [128, D], F32, kind="ExternalInput").ap()
    out_d = nc.dram_tensor("out", [tt, D], F32, kind="ExternalOutput").ap()
    projT_d = nc.dram_tensor("projT", [NCOL2, tt], F32, kind="Internal").ap()
    yT_d = [nc.dram_tensor(f"yT{h}", [512, tt], F32, kind="Internal").ap() for h in range(2)]
    mix_d = nc.dram_tensor("mix", [2048, tt], F32, kind="Internal").ap()
    x1_d = nc.dram_tensor("x1", [tt, D], F32, kind="Internal").ap()
    with ExitStack() as st:
        sb = lambda name, shape, dt: st.enter_context(nc.sbuf_tensor(name, shape, dt))
        P = Prog(nc)
        c.b_projT = [P.buf(f"projT{j}") for j in range(NT2)]
        b_yT = [[P.buf(f"yT{h}_{j}") for j in range(4)] for h in range(2)]
        c.b_yT = b_yT[0]
        c.b_const = P.buf("const")
        c.b_mix = P.buf("mix")
        b_x1 = P.buf("x1")
        b_xin = P.buf("xin")
        b_out = P.buf("outd")
        scr = sb("scr", [128, 1], F32)
        par_all = sb("par_all", [128, 4, NPAR], F32)
        cst_sb = sb("cst_sb", [128, NCST, 128], F32)
        lora_sb = sb("lora_sb", [128, 4, 512], F32)
        fg_sb = sb("fg_sb", [128, D], F32)
        ident_b = sb("ident_b", [128, 128], BF16)
        b_par = P.buf("par"); b_lora = P.buf("lora"); b_fg = P.buf("fg")
        P.dma("sp", lambda e: e.dma_start(out=par_all[:], in_=par_d[:, :, :]), b_par, writes=[b_par])
        P.dma("sp", lambda e: e.dma_start(out=cst_sb[:], in_=cst_d[:, :, :]), c.b_const, writes=[c.b_const])
        P.dma("sp", lambda e: e.dma_start(out=fg_sb[:], in_=fg_d[:, :]), b_fg, writes=[b_fg])
        P.op("dve", lambda e: e.tensor_copy(out=ident_b[:], in_=cst_sb[:, 0, :]), reads=[c.b_const], writes=[c.b_const])
        for k in range(4):
            emit_derived(P, par_all[:, k, :], b_par)
        uid = [0]

        def scoped():
            es_ = ExitStack()
            uid[0] += 1
            u = uid[0]
            sbx = lambda name, shape, dt: es_.enter_context(nc.sbuf_tensor(f"{name}_u{u}", shape, dt))
            psx = lambda name, shape, dt: es_.enter_context(nc.psum_tensor(f"{name}_u{u}", shape, dt))
            return es_, sbx, psx

        bar = lambda: P.barrier(lambda e: e.memset(scr[:], 0.0))
        for l in range(2):
            x_src, b_xsrc = (x_d, b_xin) if l == 0 else (x1_d, b_x1)
            par0 = par_all[:, 2 * l, :]
            par = {"buf": b_par, "g": par0[:, 64:80], "eps": par0[:, 80:81]}
            es_, sbx, psx = scoped()
            emit_inproj(P, c, nc, sbx, psx, x_src, wsel_d[l], projT_d, par, ident_b[:], tt=tt, nt=NT2)
            es_.close()
            bar()
            for hh in range(2):
                pk = par_all[:, 2 * l + hh, :]
                P.dma("sp", lambda e: e.dma_start(out=lora_sb[:], in_=lora_d[2 * l + hh]), b_lora, writes=[b_lora])
                es_, sbx, psx = scoped()
                emit_wkv(P, c, nc, sbx, psx, projT_d, pk, b_par, cst_sb, lora_sb, b_lora, yT_d[hh], tt=tt, tb=hh * NT, b_yT=b_yT[hh])
                es_.close()
                bar()
                uid[0] += 1
                with ExitStack() as st3:
                    emit_attn(P, c, nc, st3, projT_d, yT_d[hh], pos_d, mix_d, pk, b_par, cst_sb, scr[:], tt=tt, tb=hh * NT, b_yT=b_yT[hh],
                              mrow=hh * 1024, final=False, sfx=f"_u{uid[0]}")
                bar()
            es_, sbx, psx = scoped()
            if l == 0:
                emit_B(P, c, nc, sbx, psx, mix_d, c.b_mix, x_src, b_xsrc, wo_d[l], fg_sb[:], b_fg, par0[:, 80:81], b_par, x1_d, b_x1, final=False, tt=tt)
            else:
                emit_B(P, c, nc, sbx, psx, mix_d, c.b_mix, x_src, b_xsrc, wo_d[l], fg_sb[:], b_fg, par0[:, 80:81], b_par, out_d, b_out, final=True, tt=tt)
            es_.close()
            bar()
        P.emit(st)
        print("fused", P.stats)
    return nc


def perm_wout(wo):
    return np.ascontiguousarray(np.concatenate([wo[0:512], wo[1024:1536], wo[512:1024], wo[1536:2048]], axis=0))


def make_inputs(inputs):
    x = np.ascontiguousarray(np.asarray(inputs["x"], np.float32))
    pos = np.asarray(inputs["positions"]).astype(np.int32)
    cst = make_consts()
    fg = np.ascontiguousarray(np.broadcast_to(np.asarray(inputs["final_g"], np.float32)[None, :], (128, D)))
    wsel = [np.concatenate([make_wsel(np.asarray(inputs["w_in"][l], np.float32), hh) for hh in range(2)], axis=1) for l in range(2)]
    wo = [perm_wout(np.asarray(inputs["w_out"][l], np.float32)) for l in range(2)]
    par = np.stack([make_par(inputs, l, hh) for l in range(2) for hh in range(2)], axis=1)
    lora = np.stack([make_lora(inputs, l, hh) for l in range(2) for hh in range(2)], axis=0)
    maps = []
    for cid in range(8):
        b = cid // 2
        maps.append({"x": x[b], "wsel0": wsel[0], "wsel1": wsel[1], "wout0": wo[0], "wout1": wo[1], "par": np.ascontiguousarray(par),
                     "cst": cst, "lora": np.ascontiguousarray(lora), "pos": np.ascontiguousarray(pos[b:b + 1]), "fg": fg})
    return maps


def run_fused(inputs, run_fn):
    nc = build_fused()
    res = run_fn(nc, make_inputs(inputs))
    out = np.empty((4, T, D), np.float32)
    for b in range(4):
        out[b, :2048] = res[2 * b]["out"][:2048]
        out[b, 2048:] = res[2 * b + 1]["out"][2048:]
    return out


def kernel(**inputs):
    def run_fn(nc, maps):
        return run_bass_kernel_spmd(nc, maps, core_ids=list(range(8))).results
    return run_fused(inputs, run_fn).astype(np.float32)
```

```python
from concourse.bass_utils import run_bass_kernel_spmd
import concourse.bass as bass
import concourse.mybir as mybir

ENGS = ("pe", "act", "dve", "pool", "sp")


class Buf:
    __slots__ = ("name", "writers", "readers")

    def __init__(self, name):
        self.name = name
        self.writers = []
        self.readers = []


class Op:
    __slots__ = ("eng", "fn", "reads", "writes", "dma_key", "deps", "signal", "cnt", "idx", "phase_i", "slot")

    def __init__(self, eng, fn, reads, writes, dma_key):
        self.eng, self.fn, self.reads, self.writes, self.dma_key = eng, fn, reads, writes, dma_key
        self.deps = set()
        self.signal = False
        self.cnt = None


class _Rec:
    def __init__(self):
        self.call = None

    def __getattr__(self, name):
        def f(*a, **k):
            self.call = (name, a, k)
            return None
        return f


def _bind(fn):
    r = _Rec()
    fn(r)
    name, a, k = r.call
    return lambda e: getattr(e, name)(*a, **k)


class Prog:
    def __init__(self, nc, same_engine_sync=True):
        self.nc = nc
        self.ops = []
        self.same_engine_sync = same_engine_sync
        self.final_dma = []
        self.phase = Buf("phase")
        self.phase_i = 0

    def buf(self, name):
        return Buf(name)

    def op(self, eng, fn, reads=(), writes=()):
        o = Op(eng, _bind(fn), tuple(reads) + (self.phase,), tuple(writes), None)
        o.phase_i = self.phase_i
        self.ops.append(o)
        return o

    def barrier(self, fn):
        o = Op("dve", _bind(fn), (), (self.phase,), None)
        o.phase_i = self.phase_i
        self.phase_i += 1
        self.ops.append(o)
        return o

    def dma(self, eng, fn, key, reads=(), writes=(), final=False):
        o = Op(eng, _bind(fn), tuple(reads) + (self.phase,), tuple(writes), key)
        o.phase_i = self.phase_i
        self.ops.append(o)
        if final:
            self.final_dma.append(o)
        return o

    def emit(self, stack):
        nc = self.nc
        ops = self.ops
        for i, o in enumerate(ops):
            o.idx = i
        for o in ops:
            for b in o.reads:
                for w in b.writers:
                    o.deps.add(w)
            for b in o.writes:
                for w in b.writers:
                    o.deps.add(w)
                for r in b.readers:
                    o.deps.add(r)
            for b in o.reads:
                b.readers.append(o)
            for b in o.writes:
                if b.readers:
                    b.writers = [o]
                    b.readers = []
                else:
                    b.writers.append(o)
            o.deps.discard(o)
            best = {}
            for dd in o.deps:
                k_ = ("k", dd.dma_key, dd.phase_i) if dd.dma_key is not None else ("e", dd.eng)
                if k_ not in best or dd.idx > best[k_].idx:
                    best[k_] = dd
            o.deps = set(best.values())
        for o in ops:
            for d in o.deps:
                if d.dma_key is not None:
                    continue
                if d.eng == o.eng and (d.eng == "pe" or not self.same_engine_sync) and o.dma_key is None:
                    continue
                d.signal = True
        eng_cnt = {e: 0 for e in ENGS}
        slot_cnt = []
        key_slot = {}
        phase_nslots = {}
        slot_of = {}
        for o in ops:
            if o.dma_key is not None:
                cls = 1 if o.eng == "pool" else 0
                ks = (o.dma_key, o.phase_i, cls)
                if ks not in key_slot:
                    n = phase_nslots.get((o.phase_i, cls), 0)
                    phase_nslots[(o.phase_i, cls)] = n + 1
                    if (cls, n) not in slot_of:
                        slot_of[(cls, n)] = len(slot_cnt)
                        slot_cnt.append(0)
                    key_slot[ks] = slot_of[(cls, n)]
                o.slot = key_slot[ks]
                slot_cnt[o.slot] += 16
                o.cnt = slot_cnt[o.slot]
            elif o.signal:
                eng_cnt[o.eng] += 1
                o.cnt = eng_cnt[o.eng]
        sems = {}
        for e in ENGS:
            if eng_cnt[e] > 0:
                sems[e] = stack.enter_context(nc.semaphore("s_" + e))
        ksems = {}
        for n in range(len(slot_cnt)):
            ksems[n] = stack.enter_context(nc.semaphore(f"d_slot{n}"))
        self.n_sems = len(sems) + len(ksems)
        block = stack.enter_context(nc.Block())
        per_eng = {e: [o for o in ops if o.eng == e] for e in ENGS}
        final_dma = self.final_dma

        def gen(ename):
            def body(e):
                waited = {}
                for o in per_eng[ename]:
                    need = {}
                    for d in o.deps:
                        if d.dma_key is not None:
                            s = ("k", d.slot)
                        else:
                            if d.eng == o.eng and (d.eng == "pe" or not self.same_engine_sync) and o.dma_key is None:
                                continue
                            s = ("e", d.eng)
                        if d.cnt > need.get(s, 0):
                            need[s] = d.cnt
                    for s, v in need.items():
                        if waited.get(s, 0) >= v:
                            continue
                        waited[s] = v
                        sem = ksems[s[1]] if s[0] == "k" else sems[s[1]]
                        e.wait_ge(sem, v)
                    ins = o.fn(e)
                    if o.dma_key is not None:
                        ins.then_inc(ksems[o.slot], 16)
                    elif o.signal:
                        ins.then_inc(sems[o.eng], 1)
                fin = {}
                for o in final_dma:
                    if o.eng == ename:
                        fin[o.slot] = max(fin.get(o.slot, 0), o.cnt)
                for k, v in fin.items():
                    e.wait_ge(ksems[k], v)
            return body

        if per_eng["pe"]:
            block.tensor(gen("pe"))
        if per_eng["act"]:
            block.scalar(gen("act"))
        if per_eng["dve"]:
            block.vector(gen("dve"))
        if per_eng["pool"]:
            block.gpsimd(gen("pool"))
        if per_eng["sp"]:
            block.sync(gen("sp"))
        self.stats = {e: len(per_eng[e]) for e in ENGS}
        self.stats["sems"] = self.n_sems
        self.stats["signals"] = dict(eng_cnt)


import os
import numpy as np
from contextlib import ExitStack
import concourse.bass as bass
import concourse.mybir as mybir

F32 = mybir.dt.float32
BF16 = mybir.dt.bfloat16
I32 = mybir.dt.int32
AF = mybir.ActivationFunctionType
ALU = mybir.AluOpType

D = 2048
T = 4096
NT = 29
NCOL = NT * 128
KC = D // 128
T_R, T_K, T_V, T_XW, T_XA, T_GRW, T_Q, T_KA, T_KB, T_VAT, T_GAT = 0, 4, 8, 12, 13, 14, 18, 22, 23, 24, 25
NPAR = 128


class Ctx:
    pass


def emit_inproj(P, c, nc, sb, ps, x_d, wsel_d, projT_d, par, ident_b, tt=T, nt=NT):
    HT = 2048 if tt >= 2048 else tt
    nhalf = tt // HT
    xt = [sb(f"xt{i}", [128, D], F32) for i in range(2)]
    xb = [sb(f"xb{i}", [128, D], BF16) for i in range(2)]
    junk = sb("junk", [128, D], BF16)
    stat = [sb(f"stat{i}", [128, 2], F32) for i in range(2)]
    hT = sb("hT", [128, KC, HT], BF16)
    wt = [sb(f"wt{i}", [128, KC, 128], BF16) for i in range(3)]
    stg = [sb(f"stg{i}", [128, 512], F32) for i in range(3)]
    pT = [ps(f"pT{i}", [128, D], BF16) for i in range(2)]
    pm = [ps(f"pm{i}", [128, 512], F32) for i in range(2)]
    b_xt = [P.buf(f"xt{i}") for i in range(2)]
    b_xb = [P.buf(f"xb{i}") for i in range(2)]
    b_junk = P.buf("junk")
    b_stat = [P.buf(f"stat{i}") for i in range(2)]
    b_hT = [P.buf(f"hT{i}") for i in range(HT // 128)]
    b_wt = [P.buf(f"wt{i}") for i in range(3)]
    b_stg = [P.buf(f"stg{i}") for i in range(3)]
    b_pT = [P.buf(f"pT{i}") for i in range(2)]
    b_pm = [P.buf(f"pm{i}") for i in range(2)]
    gcol = par["g"]
    epsc = par["eps"]
    b_par = par["buf"]
    wv = wsel_d.rearrange("(kc p) c -> p kc c", p=128)
    nev = 0
    for half in range(nhalf):
        for ti in range(HT // 128):
            i = ti % 2
            tok0 = half * HT + ti * 128
            P.dma("sp", lambda e, i=i, tok0=tok0: e.dma_start(out=xt[i][:], in_=x_d[tok0:tok0 + 128, :]), b_xt[i], writes=[b_xt[i]])
            P.op("dve", lambda e, i=i: e.memset(stat[i][:], 0.0), writes=[b_stat[i]])
            P.op("act", lambda e, i=i: e.activation(out=junk[:], in_=xt[i][:], func=AF.Square, accum_out=stat[i][:, 0:1]),
                 reads=[b_xt[i]], writes=[b_junk, b_stat[i]])
            P.op("act", lambda e, i=i: e.activation(out=stat[i][:, 1:2], in_=stat[i][:, 0:1], func=AF.Sqrt, scale=1.0 / D, bias=epsc),
                 reads=[b_stat[i], b_par], writes=[b_stat[i]])
            P.op("dve", lambda e, i=i: e.reciprocal(out=stat[i][:, 1:2], in_=stat[i][:, 1:2]), reads=[b_stat[i]], writes=[b_stat[i]])
            P.op("dve", lambda e, i=i: e.tensor_scalar(out=xb[i][:], in0=xt[i][:], scalar1=stat[i][:, 1:2], scalar2=None, op0=ALU.mult),
                 reads=[b_xt[i], b_stat[i]], writes=[b_xb[i]])
            for kc in range(KC):
                P.op("pe", lambda e, i=i, kc=kc: e.transpose(out=pT[i][:, kc * 128:(kc + 1) * 128], in_=xb[i][:, kc * 128:(kc + 1) * 128], identity=ident_b),
                     reads=[b_xb[i], c.b_const], writes=[b_pT[i]])
            eng = "dve" if ti % 2 == 0 else "pool"
            eng = "dve"
            P.op(eng, lambda e, i=i, ti=ti: e.tensor_tensor(
                out=hT[:, :, ti * 128:(ti + 1) * 128],
                in0=pT[i][:].rearrange("p (k t) -> p k t", k=KC),
                in1=gcol.unsqueeze(2).to_broadcast([128, KC, 128]), op=ALU.mult),
                reads=[b_pT[i], b_par], writes=[b_hT[ti]])
        for j in range(nt):
            wi = j % 3
            P.dma("pool", lambda e, wi=wi, j=j: e.dma_start(out=wt[wi][:], in_=wv[:, :, j * 128:(j + 1) * 128]), b_wt[wi], writes=[b_wt[wi]])
            for tg in range(HT // 512):
                pi = nev % 2
                si = nev % 3
                for kc in range(KC):
                    P.op("pe", lambda e, pi=pi, wi=wi, kc=kc, tg=tg: e.matmul(pm[pi][:], lhsT=wt[wi][:, kc, :], rhs=hT[:, kc, tg * 512:(tg + 1) * 512],
                                                                          start=(kc == 0), stop=(kc == KC - 1)),
                         reads=[b_wt[wi]] + b_hT[tg * 4:(tg + 1) * 4], writes=[b_pm[pi]])
                if nev % 2 == 0:
                    P.op("act", lambda e, pi=pi, si=si: e.copy(out=stg[si][:], in_=pm[pi][:]), reads=[b_pm[pi]], writes=[b_stg[si]])
                else:
                    P.op("dve", lambda e, pi=pi, si=si: e.tensor_copy(out=stg[si][:], in_=pm[pi][:]), reads=[b_pm[pi]], writes=[b_stg[si]])
                t0 = half * HT + tg * 512
                P.dma("sp", lambda e, si=si, j=j, t0=t0: e.dma_start(out=projT_d[j * 128:(j + 1) * 128, t0:t0 + 512], in_=stg[si][:]),
                      b_stg[si], reads=[b_stg[si]], writes=[c.b_projT[j]])
                nev += 1


def emit_derived(P, par_sb, b_par):
    def c0(dst, m0, m1):
        P.op("dve", lambda e: e.tensor_tensor(out=par_sb[:, dst:dst + 1], in0=par_sb[:, m0:m0 + 1], in1=par_sb[:, m1:m1 + 1], op=ALU.add), reads=[b_par], writes=[b_par])
        P.op("dve", lambda e: e.tensor_scalar(out=par_sb[:, dst:dst + 1], in0=par_sb[:, dst:dst + 1], scalar1=-1.0, scalar2=1.0, op0=ALU.mult, op1=ALU.add), reads=[b_par], writes=[b_par])
    for i in range(4):
        pb = i * 15
        c0(96 + i * 4 + 0, pb + 0, pb + 1)
        c0(96 + i * 4 + 1, pb + 2, pb + 3)
        c0(96 + i * 4 + 2, pb + 4, pb + 5)
        P.op("dve", lambda e, i=i, pb=pb: e.tensor_scalar(out=par_sb[:, 96 + i * 4 + 3:96 + i * 4 + 4], in0=par_sb[:, pb + 11:pb + 12], scalar1=-1.0, scalar2=1.0, op0=ALU.mult, op1=ALU.add),
             reads=[b_par], writes=[b_par])
    c0(112, 60, 61)
    c0(113, 62, 63)


def build_A(stage=1, tt=T, tiles=(0, 1, 2, 3), dirs=(0, 1)):
    nc = bass.Bass("TRN2", target_bir_lowering=False)
    c = Ctx()
    x_d = nc.dram_tensor("x", [tt, D], F32, kind="ExternalInput").ap()
    wsel_d = nc.dram_tensor("wsel", [D, NCOL], F32, kind="ExternalInput").ap()
    par_d = nc.dram_tensor("par", [128, NPAR], F32, kind="ExternalInput").ap()
    cst_d = nc.dram_tensor("cst", [128, NCST, 128], F32, kind="ExternalInput").ap()
    lora_d = nc.dram_tensor("lora", [128, 4, 512], F32, kind="ExternalInput").ap()
    projT_d = nc.dram_tensor("projT", [NCOL, tt], F32, kind="ExternalOutput" if stage == 1 and tt < 4096 else "Internal").ap()
    yT_d = nc.dram_tensor("yT", [512, tt], F32, kind="ExternalOutput" if stage == 2 else "Internal").ap()
    pos_d = nc.dram_tensor("pos", [1, tt], I32, kind="ExternalInput").ap()
    mix_d = nc.dram_tensor("mix", [1024, tt], F32, kind="ExternalOutput" if stage >= 3 else "Internal").ap()
    with ExitStack() as st:
        sb = lambda name, shape, dt: st.enter_context(nc.sbuf_tensor(name, shape, dt))
        P = Prog(nc)
        c.b_projT = [P.buf(f"projT{j}") for j in range(NT)]
        c.b_yT = [P.buf(f"yT{j}") for j in range(4)]
        c.b_const = P.buf("const")
        c.b_mix = P.buf("mix")
        scr = sb("scr", [128, 1], F32)
        par_sb = sb("par_sb", [128, NPAR], F32)
        cst_sb = sb("cst_sb", [128, NCST, 128], F32)
        lora_sb = sb("lora_sb", [128, 4, 512], F32)
        ident_b = sb("ident_b", [128, 128], BF16)
        b_par = P.buf("par")
        b_lora = P.buf("lora")
        P.dma("sp", lambda e: e.dma_start(out=par_sb[:], in_=par_d[:, :]), b_par, writes=[b_par])
        P.dma("sp", lambda e: e.dma_start(out=cst_sb[:], in_=cst_d[:, :, :]), c.b_const, writes=[c.b_const])
        P.dma("sp", lambda e: e.dma_start(out=lora_sb[:], in_=lora_d[:, :, :]), b_lora, writes=[b_lora])
        P.op("dve", lambda e: e.tensor_copy(out=ident_b[:], in_=cst_sb[:, 0, :]), reads=[c.b_const], writes=[c.b_const])
        emit_derived(P, par_sb, b_par)
        par = {"buf": b_par, "g": par_sb[:, 64:80], "eps": par_sb[:, 80:81]}
        with ExitStack() as st1:
            sb1 = lambda name, shape, dt: st1.enter_context(nc.sbuf_tensor(name, shape, dt))
            ps1 = lambda name, shape, dt: st1.enter_context(nc.psum_tensor(name, shape, dt))
            emit_inproj(P, c, nc, sb1, ps1, x_d, wsel_d, projT_d, par, ident_b[:], tt=tt)
        if stage == 1:
            for o in P.ops:
                if o.dma_key is not None and any(w in c.b_projT for w in o.writes):
                    P.final_dma.append(o)
        if stage >= 2:
            P.barrier(lambda e: e.memset(scr[:], 0.0))
            with ExitStack() as st2:
                sb2 = lambda name, shape, dt: st2.enter_context(nc.sbuf_tensor(name, shape, dt))
                ps2 = lambda name, shape, dt: st2.enter_context(nc.psum_tensor(name, shape, dt))
                emit_wkv(P, c, nc, sb2, ps2, projT_d, par_sb, b_par, cst_sb, lora_sb, b_lora, yT_d, tt=tt, tiles=tiles, dirs=dirs)
            if stage == 2:
                for o in P.ops:
                    if o.dma_key is not None and any(w in c.b_yT for w in o.writes):
                        P.final_dma.append(o)
        if stage >= 3:
            P.barrier(lambda e: e.memset(scr[:], 0.0))
            with ExitStack() as st3:
                emit_attn(P, c, nc, st3, projT_d, yT_d, pos_d, mix_d, par_sb, b_par, cst_sb, scr[:], tt=tt)
        P.emit(st)
        print(P.stats)
    return nc


def col_segments(hh):
    segs = []
    for base in (0, 1024, 2048):
        for i in range(4):
            segs.append([(base + hh * 512 + i * 128, 128)])
    segs.append([(3072, 96)])
    segs.append([(3168, 96)])
    for i in range(4):
        segs.append([(3264 + hh * 512 + i * 128, 128)])
    for i in range(4):
        segs.append([(4288 + hh * 512 + i * 128, 128)])
    ka = 5312 + (2 * hh) * 64
    segs.append([(ka, 64), (ka, 64)])
    segs.append([(ka + 64, 64), (ka + 64, 64)])
    segs.append([(5568 + 2 * hh * 64, 128)])
    for i in range(4):
        segs.append([(5824 + hh * 512 + i * 128, 128)])
    assert len(segs) == NT
    return segs


def make_wsel(w_in_l, hh):
    out = np.zeros((D, NCOL), np.float32)
    for j, sg in enumerate(col_segments(hh)):
        o = j * 128
        for (s, w) in sg:
            out[:, o:o + w] = w_in_l[:, s:s + w]
            o += w
    return out


def make_consts():
    cst = np.zeros((128, NCST, 128), np.float32)
    ii = np.arange(128)
    same = (ii[:, None] // 64) == (ii[None, :] // 64)
    MUS = ((ii[:, None] < ii[None, :]) & same).astype(np.float32)
    MLS = ((ii[:, None] > ii[None, :]) & same).astype(np.float32)
    MUI = ((ii[:, None] <= ii[None, :]) & same).astype(np.float32)
    MLI = ((ii[:, None] >= ii[None, :]) & same).astype(np.float32)
    cst[:, 0] = np.eye(128)
    cst[:, 1], cst[:, 2], cst[:, 3], cst[:, 4] = MUS, MLS, MUI, MLI
    cst[:, 5] = same.astype(np.float32)
    R = np.zeros((128, 128), np.float32)
    for dp in range(128):
        hb, o = dp // 64, dp % 64
        if o < 32:
            R[hb * 64 + o + 32, dp] = -1.0
        else:
            R[hb * 64 + o - 32, dp] = 1.0
    cst[:, 6] = R
    cst[:, 8], cst[:, 9] = -MUS, -MLS
    cst[:, 10], cst[:, 11] = -MLS, -MUS
    cst[:, 12], cst[:, 13] = MUS, MUI
    cst[:, 14], cst[:, 15] = MLS, MLI
    cst[:, 20] = (ii[:, None] >= ii[None, :]).astype(np.float32)
    cst[:, 21] = (ii[:, None] <= ii[None, :]).astype(np.float32)
    cm = np.ones(512, np.float32); cm[::64] = 0.0
    cst[:, 16:20, :] = cm.reshape(1, 4, 128)
    return cst


def make_par(inp, l, hh):
    par = np.zeros((128, NPAR), np.float32)
    mu = inp["shift_mu"][l]
    for i in range(4):
        ch = hh * 512 + i * 128
        pb = i * 15
        for k_, base in enumerate((0, 1024, 2048)):
            par[:, pb + 2 * k_] = mu[0, base + ch:base + ch + 128]
            par[:, pb + 2 * k_ + 1] = mu[1, base + ch:base + ch + 128]
        for d in range(2):
            par[:, pb + 6 + d] = inp["w0"][l, d, ch:ch + 128]
            par[:, pb + 8 + d] = inp["a0"][l, d, ch:ch + 128]
        par[:, pb + 10] = inp["k_k"][l, ch:ch + 128]
        par[:, pb + 11] = inp["k_a"][l, ch:ch + 128]
        par[:, pb + 12] = inp["r_k"][l].reshape(-1)[ch:ch + 128]
        par[:, pb + 13] = inp["lnx_g"][l, ch:ch + 128]
        par[:, pb + 14] = inp["lnx_b"][l, ch:ch + 128]
    par[:96, 60] = mu[0, 3072:3168]; par[:96, 61] = mu[1, 3072:3168]
    par[:96, 62] = mu[0, 3168:3264]; par[:96, 63] = mu[1, 3168:3264]
    par[:, 64:80] = inp["norm_g"][l].reshape(16, 128).T
    par[:, 80] = 1e-6
    par[:, 81] = 64e-5
    half = 32
    inv = np.power(np.float32(10000.0), -np.arange(half, dtype=np.float32) / np.float32(half)).astype(np.float32)
    par[:, 82] = inv[np.arange(128) % 32]
    sk = inp["sink"][l]
    for g in range(2):
        h0 = hh * 8 + 4 * g
        par[:, 84 + 4 * g:88 + 4 * g] = np.array([sk[h0], sk[h0 + 2], sk[h0 + 1], sk[h0 + 3]], np.float32)[None, :]
    return par


def make_lora(inp, l, hh):
    lo = np.zeros((128, 4, 512), np.float32)
    for d in range(2):
        lo[:96, d] = inp["decay_up"][l, d][:, hh * 512:(hh + 1) * 512]
        lo[:96, 2 + d] = inp["iclr_up"][l, d][:, hh * 512:(hh + 1) * 512]
    return lo


SEG = 512
C_DECAY = -0.6065306597126334
CI_ID, CI_MUS, CI_MLS, CI_MUI, CI_MLI, CI_ONES, CI_ROT, CI_CM = 0, 1, 2, 3, 4, 5, 6, 16
NCST = 22


def emit_wkv(P, c, nc, sb, ps, projT_d, par_sb, b_par, cst_sb, lora_sb, b_lora, yT_d, tt=T, tiles=(0, 1, 2, 3), dirs=(0, 1), tb=0, b_yT=None):
    nseg = tt // SEG
    NCH = SEG // 64
    f = lambda name, cols, dt=F32: sb(name, [128, cols], dt)
    raw = {u: f("raw_" + u, SEG + 2) for u in ("r", "k", "v", "xw", "xa")}
    sh = {u: f("sh_" + u, SEG) for u in ("r", "k", "v", "xw", "xa")}
    names = ["lw", "lg", "a", "kkr", "sq", "nrm", "kk", "kdm", "kd", "b", "F", "E", "Fi", "eFi", "eE", "enFi", "rho", "al", "be", "ka", "rkd", "tmp"]
    A = {n: f("w_" + n, SEG) for n in names}
    PC = f("w_PC", NCH)
    AB = {n: f("wb_" + n, SEG, BF16) for n in ("rho", "al", "be", "ka")}
    yacc = f("yacc", tt)
    bv = f("bv", tt)
    ST = f("ST", 64)
    LANES = 2
    NN = [[f(f"NN{l}_{k}", 256, BF16) for k in range(2)] for l in range(LANES)]
    PT = [f(f"PT{l}", 128, BF16) for l in range(LANES)]
    MM = [f(f"MM{l}", 384, BF16) for l in range(LANES)]
    tok5 = [f(f"tok5_{l}", 576, BF16) for l in range(LANES)]
    nW = [f(f"nW{l}", 320, BF16) for l in range(LANES)]
    QtTs = [[f(f"QtT{g}_{l}", 128) for l in range(LANES)] for g in range(2)]
    GpTs = [[f(f"GpT{g}_{l}", 128) for l in range(LANES)] for g in range(2)]
    QtT, GpT = QtTs[0], GpTs[0]
    bank = [ps(f"bank{l}", [128, 512], F32) for l in range(LANES)]
    alt = [ps(f"alt{l}", [128, 512], F32) for l in range(LANES)]
    Hss = [[f(f"Hs{g}_{l}", 128) for l in range(LANES)] for g in range(2)]
    Hs = Hss[0]
    pp = [ps(f"pp{k}", [128, 512], F32) for k in range(2)]
    pch = [ps(f"pch{k}", [128, 512], F32) for k in range(2)]
    B = lambda n: P.buf(n)
    b_raw = {u: B("raw_" + u) for u in raw}
    b_sh = {u: B("sh_" + u) for u in sh}
    bA = {n: B("w_" + n) for n in names}
    b_PC, b_yacc, b_bv, b_ST = B("PC"), B("yacc"), B("bv"), B("ST")
    bAB = {n: B("wb_" + n) for n in ("rho", "al", "be", "ka")}
    b_NN = [[B(f"NN{l}_{k}") for k in range(2)] for l in range(LANES)]
    b_PT = [B(f"PT{l}") for l in range(LANES)]
    b_MM = [B(f"MM{l}") for l in range(LANES)]
    b_tok5 = [B(f"tok5_{l}") for l in range(LANES)]
    b_nW = [B(f"nW{l}") for l in range(LANES)]
    b_QtTs = [[B(f"QtT{g}_{l}") for l in range(LANES)] for g in range(2)]
    b_GpTs = [[B(f"GpT{g}_{l}") for l in range(LANES)] for g in range(2)]
    b_QtT, b_GpT = b_QtTs[0], b_GpTs[0]
    b_bank = [B(f"bank{l}") for l in range(LANES)]
    b_alt = [B(f"alt{l}") for l in range(LANES)]
    b_Hss = [[B(f"Hs{g}_{l}") for l in range(LANES)] for g in range(2)]
    b_Hs = b_Hss[0]
    gctr = 0
    pend = []
    nchain = [0]
    b_pp = [B(f"pp{k}") for k in range(2)]
    b_pch = [B(f"pch{k}") for k in range(2)]
    bc = c.b_const
    cst = lambda i: cst_sb[:, i, :]
    pcol = lambda j: par_sb[:, j:j + 1]

    DEFER = [None]

    def _prebind(fn):
        r = _Rec()
        fn(r)
        name, a, k = r.call
        return lambda e: getattr(e, name)(*a, **k)

    def OP(eng, fn, reads, writes):
        if DEFER[0] is not None:
            fb_ = _prebind(fn)
            reads, writes = list(reads), list(writes)
            DEFER[0].append(lambda: P.op(eng, fb_, reads=reads, writes=writes))
        else:
            P.op(eng, fn, reads=reads, writes=writes)

    def DMA(eng, fn, key, reads, writes):
        if DEFER[0] is not None:
            fb_ = _prebind(fn)
            reads, writes = list(reads), list(writes)
            DEFER[0].append(lambda: P.dma(eng, fb_, key, reads=reads, writes=writes))
        else:
            P.dma(eng, fn, key, reads=reads, writes=writes)

    DBN = ("rho", "al", "be", "ka")
    A2 = {n: [A[n], f("w2_" + n, SEG)] for n in DBN}
    bA2 = {n: [bA[n], B("w2_" + n)] for n in DBN}
    AB2 = {n: [AB[n], f("wb2_" + n, SEG, BF16)] for n in DBN}
    bAB2 = {n: [bAB[n], B("wb2_" + n)] for n in DBN}
    shv2 = [sh["v"], f("sh2_v", SEG)]
    b_shv2 = [b_sh["v"], B("sh2_v")]
    PC2 = [PC, f("w2_PC", NCH)]
    b_PC2 = [b_PC, B("PC2")]
    segbase = [0]

    def bind_par(par_):
        nonlocal PC, b_PC
        for n_ in DBN:
            A[n_] = A2[n_][par_]
            bA[n_] = bA2[n_][par_]
            AB[n_] = AB2[n_][par_]
            bAB[n_] = bAB2[n_][par_]
        sh["v"] = shv2[par_]
        b_sh["v"] = b_shv2[par_]
        PC = PC2[par_]
        b_PC = b_PC2[par_]

    for l in range(LANES):
        OP("pool", lambda e, l=l: e.memset(tok5[l][:], 0.0), [], [b_tok5[l]])
        OP("pool", lambda e, l=l: e.memset(nW[l][:], 0.0), [], [b_nW[l]])
    V0, BE0, KA0, X10, AL0 = 64, 192, 320, 448, 512
    NW1, NW2 = 64, 192
    for i in tiles:
        pb = i * 15
        for d in dirs:
            OP("dve", lambda e: e.memset(ST[:], 0.0), [], [b_ST])
            segs = list(range(nseg)) if d == 0 else list(range(nseg - 1, -1, -1))
            def prep(s):
                t0 = s * SEG
                for u, trow in (("r", tb + T_R + i), ("k", tb + T_K + i), ("v", tb + T_V + i), ("xw", tb + T_XW), ("xa", tb + T_XA)):
                    lo = t0 - 1
                    hi = t0 + SEG + 1
                    dlo, dhi = 0, SEG + 2
                    if lo < 0:
                        lo, dlo = 0, 1
                    if hi > tt:
                        hi, dhi = tt, SEG + 1
                    if dlo == 1 or dhi == SEG + 1:
                        OP("pool", lambda e, u=u: e.memset(raw[u][:], 0.0), [], [b_raw[u]])
                    DMA("sp", lambda e, u=u, trow=trow, lo=lo, hi=hi, dlo=dlo, dhi=dhi: e.dma_start(
                        out=raw[u][:, dlo:dhi], in_=projT_d[trow * 128:(trow + 1) * 128, lo:hi]),
                        b_raw[u], [c.b_projT[trow]], [b_raw[u]])
                shp = {"r": (pb + 0, pb + 1, 96 + i * 4 + 0), "k": (pb + 2, pb + 3, 96 + i * 4 + 1), "v": (pb + 4, pb + 5, 96 + i * 4 + 2),
                       "xw": (60, 61, 112), "xa": (62, 63, 113)}
                for u in ("r", "k", "v", "xw", "xa"):
                    m0, m1, c0 = shp[u]
                    OP("act", lambda e, u=u, c0=c0: e.activation(out=sh[u][:], in_=raw[u][:, 1:SEG + 1], func=AF.Copy, scale=pcol(c0)),
                       [b_raw[u], b_par], [b_sh[u]])
                    OP("dve", lambda e, u=u, m0=m0: e.scalar_tensor_tensor(out=sh[u][:], in0=raw[u][:, 0:SEG], scalar=pcol(m0), in1=sh[u][:],
                                                                              op0=ALU.mult, op1=ALU.add), [b_raw[u], b_sh[u], b_par], [b_sh[u]])
                    OP("dve", lambda e, u=u, m1=m1: e.scalar_tensor_tensor(out=sh[u][:], in0=raw[u][:, 2:SEG + 2], scalar=pcol(m1), in1=sh[u][:],
                                                                              op0=ALU.mult, op1=ALU.add), [b_raw[u], b_sh[u], b_par], [b_sh[u]])
                OP("act", lambda e: e.activation(out=A["lw"][:], in_=sh["xw"][:], func=AF.Tanh), [b_sh["xw"]], [bA["lw"]])
                OP("pe", lambda e, d=d, i=i: e.matmul(pp[0][:], lhsT=lora_sb[0:96, d, i * 128:(i + 1) * 128], rhs=A["lw"][0:96, :], start=True, stop=True),
                   [bA["lw"], b_lora], [b_pp[0]])
                OP("act", lambda e, d=d: e.activation(out=A["lg"][:], in_=pp[0][:], func=AF.Sigmoid, bias=pcol(pb + 6 + d)), [b_pp[0], b_par], [bA["lg"]])
                OP("dve", lambda e: e.tensor_scalar(out=A["lg"][:], in0=A["lg"][:], scalar1=C_DECAY, scalar2=None, op0=ALU.mult), [bA["lg"]], [bA["lg"]])
                OP("pe", lambda e, d=d, i=i: e.matmul(pp[1][:], lhsT=lora_sb[0:96, 2 + d, i * 128:(i + 1) * 128], rhs=sh["xa"][0:96, :], start=True, stop=True),
                   [b_sh["xa"], b_lora], [b_pp[1]])
                OP("act", lambda e, d=d: e.activation(out=A["a"][:], in_=pp[1][:], func=AF.Sigmoid, bias=pcol(pb + 8 + d)), [b_pp[1], b_par], [bA["a"]])
                OP("dve", lambda e: e.tensor_scalar(out=A["kkr"][:], in0=sh["k"][:], scalar1=pcol(pb + 10), scalar2=None, op0=ALU.mult),
                   [b_sh["k"], b_par], [bA["kkr"]])
                OP("pool", lambda e: e.tensor_tensor(out=A["sq"][:], in0=A["kkr"][:], in1=A["kkr"][:], op=ALU.mult), [bA["kkr"]], [bA["sq"]])
                OP("pe", lambda e: e.matmul(pp[0][:], lhsT=cst(CI_ONES), rhs=A["sq"][:], start=True, stop=True), [bA["sq"], bc], [b_pp[0]])
                OP("act", lambda e: e.activation(out=A["nrm"][:], in_=pp[0][:], func=AF.Sqrt), [b_pp[0]], [bA["nrm"]])
                OP("dve", lambda e: e.tensor_scalar(out=A["nrm"][:], in0=A["nrm"][:], scalar1=1e-12, scalar2=None, op0=ALU.max), [bA["nrm"]], [bA["nrm"]])
                OP("dve", lambda e: e.reciprocal(out=A["nrm"][:], in_=A["nrm"][:]), [bA["nrm"]], [bA["nrm"]])
                OP("dve", lambda e: e.tensor_tensor(out=A["kk"][:], in0=A["kkr"][:], in1=A["nrm"][:], op=ALU.mult), [bA["kkr"], bA["nrm"]], [bA["kk"]])
                OP("dve", lambda e: e.tensor_scalar(out=A["kdm"][:], in0=A["a"][:], scalar1=pcol(pb + 11), scalar2=pcol(96 + i * 4 + 3), op0=ALU.mult, op1=ALU.add),
                   [bA["a"], b_par], [bA["kdm"]])
                OP("pool", lambda e: e.tensor_tensor(out=A["kd"][:], in0=sh["k"][:], in1=A["kdm"][:], op=ALU.mult), [b_sh["k"], bA["kdm"]], [bA["kd"]])
                OP("pool", lambda e: e.tensor_tensor(out=A["b"][:], in0=A["a"][:], in1=A["kk"][:], op=ALU.mult), [bA["a"], bA["kk"]], [bA["b"]])
                OP("dve", lambda e: e.tensor_tensor_scan(out=A["F"][:], data0=cst_sb[:, CI_CM:CI_CM + 4, :].rearrange("p a b -> p (a b)"), data1=A["lg"][:], initial=0.0,
                                                         op0=ALU.mult, op1=ALU.add), [bA["lg"], bc], [bA["F"]])
                F3 = A["F"][:].rearrange("p (c t) -> p c t", t=64)
                tot_bc = F3[:, :, 63:64].to_broadcast([128, NCH, 64])
                v3 = lambda n: A[n][:].rearrange("p (c t) -> p c t", t=64)
                if d == 0:
                    OP("dve", lambda e: e.tensor_tensor(out=A["E"][:], in0=A["F"][:], in1=A["lg"][:], op=ALU.subtract), [bA["F"], bA["lg"]], [bA["E"]])
                    Fi, bFi = A["F"], bA["F"]
                else:
                    OP("dve", lambda e: e.tensor_tensor(out=v3("E"), in0=tot_bc, in1=F3, op=ALU.subtract), [bA["F"]], [bA["E"]])
                    OP("dve", lambda e: e.tensor_tensor(out=A["Fi"][:], in0=A["E"][:], in1=A["lg"][:], op=ALU.add), [bA["E"], bA["lg"]], [bA["Fi"]])
                    Fi, bFi = A["Fi"], bA["Fi"]
                OP("act", lambda e, Fi=Fi: e.activation(out=A["eFi"][:], in_=Fi[:], func=AF.Exp), [bFi], [bA["eFi"]])
                OP("act", lambda e: e.activation(out=A["eE"][:], in_=A["E"][:], func=AF.Exp), [bA["E"]], [bA["eE"]])
                OP("act", lambda e, Fi=Fi: e.activation(out=A["enFi"][:], in_=Fi[:], func=AF.Exp, scale=-1.0), [bFi], [bA["enFi"]])
                OP("act", lambda e: e.activation(out=PC[:].unsqueeze(2), in_=F3[:, :, 63:64], func=AF.Exp), [bA["F"]], [b_PC])
                OP("dve", lambda e: e.tensor_tensor(out=A["rho"][:], in0=sh["r"][:], in1=A["eFi"][:], op=ALU.mult), [b_sh["r"], bA["eFi"]], [bA["rho"]])
                OP("pool", lambda e: e.tensor_tensor(out=A["al"][:], in0=A["kk"][:], in1=A["eE"][:], op=ALU.mult), [bA["kk"], bA["eE"]], [bA["al"]])
                OP("dve", lambda e: e.tensor_tensor(out=A["be"][:], in0=A["b"][:], in1=A["enFi"][:], op=ALU.mult), [bA["b"], bA["enFi"]], [bA["be"]])
                OP("pool", lambda e: e.tensor_tensor(out=A["ka"][:], in0=A["kd"][:], in1=A["enFi"][:], op=ALU.mult), [bA["kd"], bA["enFi"]], [bA["ka"]])
                for n_ in ("rho", "al", "be", "ka"):
                    OP("act", lambda e, n_=n_: e.copy(out=AB[n_][:], in_=A[n_][:]), [bA[n_]], [bAB[n_]])
                OP("dve", lambda e: e.scalar_tensor_tensor(out=A["rkd"][:], in0=sh["r"][:], scalar=pcol(pb + 12), in1=A["kd"][:], op0=ALU.mult, op1=ALU.mult),
                   [b_sh["r"], bA["kd"], b_par], [bA["rkd"]])
                OP("pe", lambda e: e.matmul(pp[1][:], lhsT=cst(CI_ONES), rhs=A["rkd"][:], start=True, stop=True), [bA["rkd"], bc], [b_pp[1]])
                if d == dirs[0]:
                    OP("dve", lambda e, t0=t0: e.tensor_tensor(out=bv[:, t0:t0 + SEG], in0=pp[1][:], in1=sh["v"][:], op=ALU.mult), [b_pp[1], b_sh["v"]], [b_bv])
                else:
                    OP("dve", lambda e: e.tensor_tensor(out=A["tmp"][:], in0=pp[1][:], in1=sh["v"][:], op=ALU.mult), [b_pp[1], b_sh["v"]], [bA["tmp"]])
                    OP("pool", lambda e, t0=t0: e.tensor_tensor(out=bv[:, t0:t0 + SEG], in0=bv[:, t0:t0 + SEG], in1=A["tmp"][:], op=ALU.add), [bA["tmp"], b_bv], [b_bv])
            def packs_phase(s):
                nonlocal gctr
                t0 = s * SEG
                packs = list(range(SEG // 128)) if d == 0 else list(range(SEG // 128 - 1, -1, -1))
                mA, mB = (8, 9) if d == 0 else (9, 8)
                mS, mI = (CI_MUS, CI_MUI) if d == 0 else (CI_MLS, CI_MLI)
                for hb in range(2):
                    hs = slice(hb * 64, hb * 64 + 64)
                    rd_in = [bA["rho"], bA["al"], bA["be"], bA["ka"]]

                    def fm(n, p):
                        return A[n][hs, p * 128:(p + 1) * 128]

                    def fb(n, p):
                        return AB[n][hs, p * 128:(p + 1) * 128]
                    rd_b = [bAB["rho"], bAB["al"], bAB["be"], bAB["ka"]]

                    def pad(tl, off, rows=slice(0, 128)):
                        return tl[rows, off:off + 128] if hb == 0 else tl[rows, off - 64:off + 64]

                    steps = []
                    def s1(l, p):
                        OP("pe", lambda e: e.matmul(bank[l][:, 0:128], lhsT=fb("be", p), rhs=fb("al", p), start=True, stop=True), rd_b, [b_bank[l]])
                        OP("pe", lambda e: e.matmul(bank[l][:, 128:256], lhsT=fb("al", p), rhs=fb("be", p), start=True, stop=True), rd_b, [b_bank[l]])
                        OP("pe", lambda e: e.matmul(bank[l][:, 256:384], lhsT=fb("ka", p), rhs=fb("al", p), start=True, stop=True), rd_b, [b_bank[l]])
                        OP("pe", lambda e: e.matmul(bank[l][:, 384:512], lhsT=fb("be", p), rhs=fb("rho", p), start=True, stop=True), rd_b, [b_bank[l]])
                        OP("dve", lambda e: e.tensor_tensor(out=NN[l][0][:].rearrange("p (a b) -> p a b", a=2), in0=bank[l][:, 0:256].rearrange("p (a b) -> p a b", a=2),
                                                            in1=cst_sb[:, 8:10, :] if d == 0 else cst_sb[:, 10:12, :], op=ALU.mult), [b_bank[l], bc], [b_NN[l][0]])
                        OP("dve", lambda e: e.tensor_tensor(out=MM[l][:, 0:256].rearrange("p (a b) -> p a b", a=2), in0=bank[l][:, 256:512].rearrange("p (a b) -> p a b", a=2),
                                                            in1=cst_sb[:, 12:14, :] if d == 0 else cst_sb[:, 14:16, :], op=ALU.mult), [b_bank[l], bc], [b_MM[l]])
                    steps.append(s1)
                    def s2(l, p):
                        S2V = 0
                        OP("pe", lambda e: e.matmul(alt[l][:, 0:128], lhsT=fb("ka", p), rhs=fb("rho", p), start=True, stop=True), rd_b, [b_alt[l]])
                        if S2V == 1:
                            OP("dve", lambda e: e.tensor_tensor(out=MM[l][:, 256:384], in0=alt[l][:, 0:128], in1=cst(mI), op=ALU.mult), [b_alt[l], bc], [b_MM[l]])
                            OP("dve", lambda e: e.tensor_tensor(out=PT[l][:], in0=NN[l][0][:, 0:128], in1=cst(CI_ID), op=ALU.add), [b_NN[l][0], bc], [b_PT[l]])
                            return
                        if S2V == 2:
                            for k_, src in enumerate((sh["v"], A["be"], A["ka"], None, A["al"])):
                                if src is None:
                                    continue
                                OP("pe", lambda e, k_=k_, src=src: e.transpose(out=alt[l][:, 128 + k_ * 64:128 + (k_ + 1) * 64], in_=src[hs, p * 128:(p + 1) * 128],
                                                                             identity=cst_sb[hs, CI_ID, hb * 64:hb * 64 + 64]),
                                   [b_sh["v"], bA["be"], bA["ka"], bA["al"], bc], [b_alt[l]])
                            OP("dve", lambda e: e.tensor_tensor(out=MM[l][:, 256:384], in0=alt[l][:, 0:128], in1=cst(mI), op=ALU.mult), [b_alt[l], bc], [b_MM[l]])
                            return
                        if S2V == 3:
                            OP("dve", lambda e: e.tensor_tensor(out=PT[l][:], in0=NN[l][0][:, 0:128], in1=cst(CI_ID), op=ALU.add), [b_NN[l][0], bc], [b_PT[l]])
                            return
                        if S2V == 4:
                            OP("act", lambda e: e.copy(out=tok5[l][:, AL0:AL0 + 64], in_=alt[l][:, 0:64]), [b_alt[l]], [b_tok5[l]])
                            return
                        if S2V == 5:
                            OP("act", lambda e: e.copy(out=tok5[l][:, 64:448].rearrange("p (a b) -> p a b", b=128)[:, :, 0:64],
                                                       in_=alt[l][:, 128:320].rearrange("p (a b) -> p a b", b=64)), [b_alt[l]], [b_tok5[l]])
                            return
                        for k_, src in enumerate((sh["v"], A["be"], A["ka"], None, A["al"])):
                            if src is None:
                                continue
                            OP("pe", lambda e, k_=k_, src=src: e.transpose(out=alt[l][:, 128 + k_ * 64:128 + (k_ + 1) * 64], in_=src[hs, p * 128:(p + 1) * 128],
                                                                         identity=cst_sb[hs, CI_ID, hb * 64:hb * 64 + 64]),
                               [b_sh["v"], bA["be"], bA["ka"], bA["al"], bc], [b_alt[l]])
                        OP("dve", lambda e: e.tensor_tensor(out=MM[l][:, 256:384], in0=alt[l][:, 0:128], in1=cst(mI), op=ALU.mult), [b_alt[l], bc], [b_MM[l]])
                        OP("dve", lambda e: e.tensor_copy(out=tok5[l][:, 64:448].rearrange("p (a b) -> p a b", b=128)[:, :, 0:64],
                                                   in_=alt[l][:, 128:320].rearrange("p (a b) -> p a b", b=64)), [b_alt[l]], [b_tok5[l]])
                        OP("dve", lambda e: e.tensor_copy(out=tok5[l][:, AL0:AL0 + 64], in_=alt[l][:, 384:448]), [b_alt[l]], [b_tok5[l]])
                        OP("dve", lambda e: e.tensor_tensor(out=PT[l][:], in0=NN[l][0][:, 0:128], in1=cst(CI_ID), op=ALU.add), [b_NN[l][0], bc], [b_PT[l]])
                    steps.append(s2)
                    for lev in range(5):
                        def s3(l, p, lev=lev):
                            cur, nxt = NN[l][lev % 2], NN[l][(lev + 1) % 2]
                            bcur, bnxt = b_NN[l][lev % 2], b_NN[l][(lev + 1) % 2]
                            OP("pe", lambda e: e.matmul(bank[l][:, 128:256], lhsT=cur[:, 0:128], rhs=cur[:, 128:256], start=True, stop=True), [bcur], [b_bank[l]])
                            OP("pe", lambda e: e.matmul(bank[l][:, 0:128], lhsT=cur[:, 128:256], rhs=cur[:, 0:128], start=True, stop=True), [bcur], [b_bank[l]])
                            OP("act", lambda e: e.copy(out=nxt[:, 0:256], in_=bank[l][:, 0:256]), [b_bank[l]], [bnxt])
                        def s4(l, p, lev=lev):
                            nxt, bnxt = NN[l][(lev + 1) % 2], b_NN[l][(lev + 1) % 2]
                            OP("pe", lambda e: e.matmul(alt[l][:, 256:384], lhsT=nxt[:, 128:256], rhs=PT[l][:], start=True, stop=True), [bnxt, b_PT[l]], [b_alt[l]])
                            OP("dve", lambda e: e.tensor_tensor(out=PT[l][:], in0=alt[l][:, 256:384], in1=PT[l][:], op=ALU.add), [b_alt[l], b_PT[l]], [b_PT[l]])
                        steps.append(s3)
                        steps.append(s4)
                    def s5(l, p):
                        OP("pe", lambda e: e.matmul(bank[l][:, 0:64], lhsT=MM[l][:, 0:128], rhs=tok5[l][:, V0:V0 + 64], start=True, stop=True), [b_MM[l], b_tok5[l]], [b_bank[l]])
                        OP("act", lambda e: e.copy(out=tok5[l][:, X10:X10 + 64], in_=bank[l][:, 0:64]), [b_bank[l]], [b_tok5[l]])
                    steps.append(s5)
                    def s6(l, p):
                        OP("pe", lambda e: e.matmul(bank[l][:, 64:192], lhsT=PT[l][:], rhs=tok5[l][:, X10:X10 + 128], start=True, stop=True), [b_PT[l], b_tok5[l]], [b_bank[l]])
                        OP("act", lambda e: e.activation(out=nW[l][:, 64:320].rearrange("p (a b) -> p a b", b=128)[:, :, 0:64], in_=bank[l][:, 64:192].rearrange("p (a b) -> p a b", b=64), func=AF.Copy, scale=-1.0), [b_bank[l]], [b_nW[l]])
                    steps.append(s6)
                    def s7(l, p):
                        tk = p * 128
                        OP("pe", lambda e: e.matmul(bank[l][:, 192:320], lhsT=pad(tok5[l], V0), rhs=MM[l][:, 256:384], start=True, stop=False), [b_tok5[l], b_MM[l]], [b_bank[l]])
                        OP("pe", lambda e: e.matmul(bank[l][:, 192:320], lhsT=pad(nW[l], NW1), rhs=MM[l][:, 128:256], start=False, stop=True), [b_nW[l], b_MM[l]], [b_bank[l]])
                        OP("pe", lambda e: e.matmul(bank[l][:, 320:448], lhsT=pad(nW[l], NW2), rhs=MM[l][:, 128:256], start=True, stop=True), [b_nW[l], b_MM[l]], [b_bank[l]])
                        if d == dirs[0]:
                            OP("dve", lambda e: e.tensor_copy(out=yacc[hs, t0 + tk:t0 + tk + 128], in_=bank[l][hs, 192:320]), [b_bank[l]], [b_yacc])
                        else:
                            OP("dve", lambda e: e.tensor_tensor(out=yacc[hs, t0 + tk:t0 + tk + 128], in0=bank[l][hs, 192:320], in1=yacc[hs, t0 + tk:t0 + tk + 128], op=ALU.add),
                               [b_bank[l], b_yacc], [b_yacc])
                        OP("dve", lambda e: e.tensor_tensor(out=QtT[l][hs, :], in0=bank[l][hs, 320:448], in1=fm("rho", p), op=ALU.add), [b_bank[l], bA["rho"]], [b_QtT[l]])
                    steps.append(s7)
                    def s8(l, p):
                        for cc in range(2):
                            cs = slice(cc * 64, cc * 64 + 64)
                            tgt, btgt = (bank[l], b_bank[l]) if cc == 0 else (alt[l], b_alt[l])
                            ch = p * 2 + cc
                            OP("pe", lambda e, cs=cs, tgt=tgt: e.matmul(tgt[:, 0:64], lhsT=pad(nW[l], NW2, cs), rhs=tok5[l][cs, BE0:BE0 + 64], start=True, stop=True),
                               [b_nW[l], b_tok5[l]], [btgt])
                            OP("pe", lambda e, cs=cs, tgt=tgt: e.matmul(tgt[:, 64:128], lhsT=pad(tok5[l], KA0, cs), rhs=tok5[l][cs, V0:V0 + 64], start=True, stop=False),
                               [b_tok5[l]], [btgt])
                            OP("pe", lambda e, cs=cs, tgt=tgt: e.matmul(tgt[:, 64:128], lhsT=pad(tok5[l], BE0, cs), rhs=nW[l][cs, NW1:NW1 + 64], start=False, stop=True),
                               [b_tok5[l], b_nW[l]], [btgt])
                            OP("dve", lambda e, cc=cc, tgt=tgt: e.tensor_tensor(out=GpT[l][hs, cc * 64:cc * 64 + 64], in0=tgt[hs, 0:64],
                                                                in1=cst_sb[hs, CI_ID, hb * 64:hb * 64 + 64], op=ALU.add),
                               [btgt, bc], [b_GpT[l]])
                            OP("dve", lambda e, cc=cc, tgt=tgt, ch=ch: e.tensor_scalar(out=Hs[l][hs, cc * 64:cc * 64 + 64], in0=tgt[hs, 64:128], scalar1=PC[hs, ch:ch + 1], scalar2=None, op0=ALU.mult),
                               [btgt, b_PC], [b_Hs[l]])
                    steps.append(s8)
                    STOP = 99
                    for g0 in range(0, len(packs), LANES):
                        grp = packs[g0:g0 + LANES]
                        gp = gctr % 2
                        gctr += 1
                        QtT, GpT, Hs = QtTs[gp], GpTs[gp], Hss[gp]
                        b_QtT, b_GpT, b_Hs = b_QtTs[gp], b_GpTs[gp], b_Hss[gp]
                        for si_, stp in enumerate(steps):
                            for l, p in enumerate(grp):
                                stp(l, p)
                            if pend and si_ % 3 == 2:
                                pend.pop(0)()
                            if dprep:
                                dprep.pop(0)()
                        while pend:
                            pend.pop(0)()
                        for l, p in enumerate(grp):
                            for cc in ((0, 1) if d == 0 else (1, 0)):
                                def chain_step(l=l, p=p, cc=cc, hs=hs, QtT=QtT, GpT=GpT, Hs=Hs, b_QtT=b_QtT, b_GpT=b_GpT, b_Hs=b_Hs, PC=PC, b_PC=b_PC, t0=t0):
                                    cs = slice(cc * 64, cc * 64 + 64)
                                    k2 = nchain[0] % 2
                                    nchain[0] += 1
                                    tk = t0 + p * 128 + cc * 64
                                    ch = p * 2 + cc
                                    OP("pe", lambda e: e.matmul(pch[k2][hs, 0:64], lhsT=ST[hs, :], rhs=QtT[l][hs, cs], start=True, stop=True), [b_ST, b_QtT[l]], [b_pch[k2]])
                                    OP("pe", lambda e: e.matmul(pch[k2][hs, 64:128], lhsT=GpT[l][hs, cs], rhs=ST[hs, :], start=True, stop=True), [b_ST, b_GpT[l]], [b_pch[k2]])
                                    OP("dve", lambda e: e.tensor_tensor(out=yacc[hs, tk:tk + 64], in0=pch[k2][hs, 0:64], in1=yacc[hs, tk:tk + 64], op=ALU.add), [b_pch[k2], b_yacc], [b_yacc])
                                    OP("dve", lambda e: e.scalar_tensor_tensor(out=ST[hs, :], in0=pch[k2][hs, 64:128], scalar=PC[hs, ch:ch + 1], in1=Hs[l][hs, cs], op0=ALU.mult, op1=ALU.add),
                                       [b_pch[k2], b_PC, b_Hs[l]], [b_ST])
                                pend.append(chain_step)
                while dprep:
                    dprep.pop(0)()

            dprep = []
            for k_seg, s in enumerate(segs):
                par_k = (segbase[0] + k_seg) % 2
                if k_seg == 0:
                    bind_par(par_k)
                    prep(s)
                if k_seg + 1 < len(segs):
                    bind_par(1 - par_k)
                    DEFER[0] = dprep
                    prep(segs[k_seg + 1])
                    DEFER[0] = None
                bind_par(par_k)
                packs_phase(s)
            while pend:
                pend.pop(0)()
            segbase[0] += len(segs)
        for s in range(nseg):
            t0 = s * SEG
            ys = yacc[:, t0:t0 + SEG]
            OP("pe", lambda e, ys=ys: e.matmul(pp[0][:], lhsT=cst(CI_ONES), rhs=ys, start=True, stop=True), [b_yacc, bc], [b_pp[0]])
            OP("dve", lambda e, ys=ys: e.scalar_tensor_tensor(out=A["tmp"][:], in0=pp[0][:], scalar=-1.0 / 64, in1=ys, op0=ALU.mult, op1=ALU.add), [b_pp[0], b_yacc], [bA["tmp"]])
            OP("pool", lambda e: e.tensor_tensor(out=A["sq"][:], in0=A["tmp"][:], in1=A["tmp"][:], op=ALU.mult), [bA["tmp"]], [bA["sq"]])
            OP("pe", lambda e: e.matmul(pp[1][:], lhsT=cst(CI_ONES), rhs=A["sq"][:], start=True, stop=True), [bA["sq"], bc], [b_pp[1]])
            OP("act", lambda e: e.activation(out=A["nrm"][:], in_=pp[1][:], func=AF.Sqrt, scale=1.0 / 64, bias=par_sb[:, 81:82]), [b_pp[1], b_par], [bA["nrm"]])
            OP("dve", lambda e: e.reciprocal(out=A["nrm"][:], in_=A["nrm"][:]), [bA["nrm"]], [bA["nrm"]])
            OP("dve", lambda e: e.tensor_tensor(out=A["tmp"][:], in0=A["tmp"][:], in1=A["nrm"][:], op=ALU.mult), [bA["tmp"], bA["nrm"]], [bA["tmp"]])
            OP("dve", lambda e: e.tensor_scalar(out=A["tmp"][:], in0=A["tmp"][:], scalar1=pcol(pb + 13), scalar2=pcol(pb + 14), op0=ALU.mult, op1=ALU.add), [bA["tmp"], b_par], [bA["tmp"]])
            OP("pool", lambda e, t0=t0: e.tensor_tensor(out=A["rkd"][:], in0=A["tmp"][:], in1=bv[:, t0:t0 + SEG], op=ALU.add), [bA["tmp"], b_bv], [bA["rkd"]])
            P.dma("sp", lambda e, t0=t0, i=i: e.dma_start(out=yT_d[i * 128:(i + 1) * 128, t0:t0 + SEG], in_=A["rkd"][:]), bA["rkd"], reads=[bA["rkd"]], writes=[(b_yT or c.b_yT)[i]])


CI_TGE, CI_TLE = 20, 21
TWO_PI = 6.283185307179586
C1_2PI = 6.28125
C2_2PI = TWO_PI - 6.28125
PI_LO = 3.1415925


def emit_attn(P, c, nc, st_outer, projT_d, yT_d, pos_d, mix_d, par_sb, b_par, cst_sb, scratch, tt=T, tb=0, b_yT=None, mrow=0, final=True, sfx=""):
    NB = tt // 128
    stA = ExitStack()
    stB = ExitStack()
    stC = ExitStack()
    cur = [st_outer]
    sb = lambda name, shape, dt: cur[0].enter_context(nc.sbuf_tensor(name + sfx, shape, dt))
    ps = lambda name, shape, dt: cur[0].enter_context(nc.psum_tensor(name + sfx, shape, dt))
    f = lambda name, cols, dt=F32: sb(name, [128, cols], dt)
    B = lambda n: P.buf(n)
    qr = sb("qr", [128, 4, tt], BF16)
    kr = sb("kr", [128, 2, tt], BF16)
    Vd = sb("Vd", [128, NB, 2, 128], BF16)
    ones_bf = sb("ones_bf", [128, 128], BF16)
    es = f("es", 8)
    cur[0] = stB
    cosT = f("cosT", tt); sinT = f("sinT", tt)
    cur[0] = stA
    bc = c.b_const

    def OP(eng, fn, reads, writes):
        P.op(eng, fn, reads=reads, writes=writes)

    posi = sb("posi", [128, tt], I32)
    ang = f("ang", tt); nn_ = f("nn_", tt); rr = f("rr", tt)
    msk = nn_
    ni = posi
    b_posi, b_ang, b_nn, b_rr, b_cos, b_sin = [B(n) for n in "posi ang nn rr cos sin".split()]
    b_msk, b_ni = b_nn, b_posi
    P.dma("sp", lambda e: e.dma_start(out=posi[:], in_=pos_d[0:1, :].partition_broadcast(128)), b_posi, writes=[b_posi])
    OP("dve", lambda e: e.tensor_copy(out=ang[:], in_=posi[:]), [b_posi], [b_ang])
    OP("dve", lambda e: e.tensor_scalar(out=ang[:], in0=ang[:], scalar1=par_sb[:, 82:83], scalar2=None, op0=ALU.mult), [b_ang, b_par], [b_ang])

    def reduce_to(dst, b_dst, src, b_src):
        OP("dve", lambda e: e.tensor_scalar(out=nn_[:], in0=src[:], scalar1=1.0 / TWO_PI, scalar2=None, op0=ALU.mult), [b_src], [b_nn])
        OP("dve", lambda e: e.tensor_copy(out=ni[:], in_=nn_[:]), [b_nn], [b_ni])
        OP("dve", lambda e: e.tensor_copy(out=nn_[:], in_=ni[:]), [b_ni], [b_nn])
        OP("dve", lambda e: e.scalar_tensor_tensor(out=dst[:], in0=nn_[:], scalar=-C1_2PI, in1=src[:], op0=ALU.mult, op1=ALU.add), [b_nn, b_src], [b_dst])
        OP("dve", lambda e: e.scalar_tensor_tensor(out=dst[:], in0=nn_[:], scalar=-C2_2PI, in1=dst[:], op0=ALU.mult, op1=ALU.add), [b_nn, b_dst], [b_dst])
        for _ in range(2):
            OP("dve", lambda e: e.tensor_scalar(out=msk[:], in0=dst[:], scalar1=3.14159265, scalar2=None, op0=ALU.is_gt), [b_dst], [b_msk])
            OP("dve", lambda e: e.scalar_tensor_tensor(out=dst[:], in0=msk[:], scalar=-TWO_PI, in1=dst[:], op0=ALU.mult, op1=ALU.add), [b_msk, b_dst], [b_dst])
            OP("dve", lambda e: e.tensor_scalar(out=msk[:], in0=dst[:], scalar1=-3.14159265, scalar2=None, op0=ALU.is_lt), [b_dst], [b_msk])
            OP("dve", lambda e: e.scalar_tensor_tensor(out=dst[:], in0=msk[:], scalar=TWO_PI, in1=dst[:], op0=ALU.mult, op1=ALU.add), [b_msk, b_dst], [b_dst])
        OP("dve", lambda e: e.tensor_scalar(out=dst[:], in0=dst[:], scalar1=PI_LO, scalar2=-PI_LO, op0=ALU.min, op1=ALU.max), [b_dst], [b_dst])

    reduce_to(rr, b_rr, ang, b_ang)
    OP("act", lambda e: e.activation(out=sinT[:], in_=rr[:], func=AF.Sin), [b_rr], [b_sin])
    OP("dve", lambda e: e.tensor_scalar(out=ang[:], in0=rr[:], scalar1=1.5707963267948966, scalar2=None, op0=ALU.add), [b_rr], [b_ang])
    reduce_to(rr, b_rr, ang, b_ang)
    OP("act", lambda e: e.activation(out=cosT[:], in_=rr[:], func=AF.Sin), [b_rr], [b_cos])

    stA.close()
    P.barrier(lambda e: e.memset(scratch, 0.0))
    cur[0] = stB
    rawq = [f(f"rawq{k}", tt) for k in range(2)]
    t1 = [f(f"rp_t1_{k}", 512) for k in range(2)]
    t2 = [f(f"rp_t2_{k}", 512) for k in range(2)]
    pp = [ps(f"app{k}", [128, 512], F32) for k in range(2)]
    b_qr = [B(f"qr{k}") for k in range(4)]
    b_kr = [B(f"kr{k}") for k in range(2)]
    b_Vd, b_ones, b_es = B("Vd"), B("ones_bf"), B("es")
    b_rawq = [B(f"rawq{k}") for k in range(2)]
    b_t1 = [B(f"rp_t1_{k}") for k in range(2)]
    b_t2 = [B(f"rp_t2_{k}") for k in range(2)]
    b_pp = [B(f"app{k}") for k in range(2)]
    OP("dve", lambda e: e.memset(ones_bf[:], 1.0), [], [b_ones])
    OP("act", lambda e: e.activation(out=es[:], in_=par_sb[:, 84:92], func=AF.Exp), [b_par], [b_es])
    cnt = 0
    for kind, k_, trow in [("q", 0, tb + T_Q), ("q", 1, tb + T_Q + 1), ("q", 2, tb + T_Q + 2), ("q", 3, tb + T_Q + 3), ("k", 0, tb + T_KA), ("k", 1, tb + T_KB)]:
        ri = cnt % 2
        cnt += 1
        P.dma("sp", lambda e: e.dma_start(out=rawq[ri][:], in_=projT_d[trow * 128:(trow + 1) * 128, :]), b_rawq[ri], reads=[c.b_projT[trow]], writes=[b_rawq[ri]])
        dst = qr[:, k_, :] if kind == "q" else kr[:, k_, :]
        bdst = b_qr[k_] if kind == "q" else b_kr[k_]
        for tg in range(tt // 512):
            pi = tg % 2
            sl = slice(tg * 512, (tg + 1) * 512)
            OP("pe", lambda e: e.matmul(pp[pi][:], lhsT=cst_sb[:, CI_ROT, :], rhs=rawq[ri][:, sl], start=True, stop=True), [b_rawq[ri], bc], [b_pp[pi]])
            OP("dve", lambda e: e.tensor_tensor(out=t2[pi][:], in0=pp[pi][:], in1=sinT[:, sl], op=ALU.mult), [b_pp[pi], b_sin], [b_t2[pi]])
            OP("pool", lambda e: e.tensor_tensor(out=t1[pi][:], in0=rawq[ri][:, sl], in1=cosT[:, sl], op=ALU.mult), [b_rawq[ri], b_cos], [b_t1[pi]])
            OP("dve", lambda e: e.tensor_tensor(out=dst[:, sl], in0=t1[pi][:], in1=t2[pi][:], op=ALU.add), [b_t1[pi], b_t2[pi]], [bdst])
    ri = cnt % 2
    P.dma("sp", lambda e: e.dma_start(out=rawq[ri][:], in_=projT_d[(tb + T_VAT) * 128:(tb + T_VAT + 1) * 128, :]), b_rawq[ri], reads=[c.b_projT[tb + T_VAT]], writes=[b_rawq[ri]])
    for n in range(NB):
        pi = n % 2
        OP("pe", lambda e: e.transpose(out=pp[pi][:, 0:128], in_=rawq[ri][:, n * 128:(n + 1) * 128], identity=cst_sb[:, CI_ID, :]), [b_rawq[ri], bc], [b_pp[pi]])
        for g in range(2):
            OP("dve", lambda e: e.tensor_copy(out=Vd[:, n, g, :].rearrange("p (a b) -> p a b", a=2), in_=pp[pi][:, g * 64:(g + 1) * 64].unsqueeze(1).to_broadcast([128, 2, 64])),
               [b_pp[pi]], [b_Vd])

    stB.close()
    P.barrier(lambda e: e.memset(scratch, 0.0))
    cur[0] = stC
    sA = [ps(f"sA{k}", [128, 512], F32) for k in range(2)]
    sB = [ps(f"sB{k}", [128, 512], F32) for k in range(2)]
    po = ps("po", [128, 512], F32)
    pd = ps("pd", [128, 512], F32)
    Pt = [sb(f"Pt{k}", [128, 512], BF16) for k in range(3)]
    den = f("den", 512); onr = f("onr", 512)
    ya = [f(f"ya{k}", tt) for k in range(2)]
    gt = [f(f"gt{k}", tt) for k in range(2)]
    c.stC = stC
    b_sA = [B(f"sA{k}") for k in range(2)]
    b_sB = [B(f"sB{k}") for k in range(2)]
    b_po, b_pd, b_den, b_onr = B("po"), B("pd"), B("den"), B("onr")
    b_Pt = [B(f"Pt{k}") for k in range(3)]
    b_ya = [B(f"ya{k}") for k in range(2)]
    b_gt = [B(f"gt{k}") for k in range(2)]
    nsc = 0
    for g in range(2):
        for n in range(NB):
            kbs = [kb for kb in (n - 1, n, n + 1) if 0 <= kb < NB]
            qs = slice(n * 128, (n + 1) * 128)
            for idx, kb in enumerate(kbs):
                k2 = nsc % 2
                nsc += 1
                ks = slice(kb * 128, (kb + 1) * 128)
                OP("pe", lambda e: e.matmul(sA[k2][:, 0:256], lhsT=kr[0:64, g, ks], rhs=qr[0:64, 2 * g:2 * g + 2, qs], start=True, stop=True),
                   [b_kr[g], b_qr[2 * g], b_qr[2 * g + 1]], [b_sA[k2]])
                OP("pe", lambda e: e.matmul(sB[k2][:, 0:256], lhsT=kr[64:128, g, ks], rhs=qr[64:128, 2 * g:2 * g + 2, qs], start=True, stop=True),
                   [b_kr[g], b_qr[2 * g], b_qr[2 * g + 1]], [b_sB[k2]])
                OP("act", lambda e: e.activation(out=Pt[idx][:, 0:256], in_=sA[k2][:, 0:256], func=AF.Exp, scale=0.125), [b_sA[k2]], [b_Pt[idx]])
                OP("act", lambda e: e.activation(out=Pt[idx][:, 256:512], in_=sB[k2][:, 0:256], func=AF.Exp, scale=0.125), [b_sB[k2]], [b_Pt[idx]])
                if kb != n:
                    mi = CI_TGE if kb < n else CI_TLE
                    OP("dve", lambda e: e.tensor_tensor(out=Pt[idx][:].rearrange("p (a b) -> p a b", a=4), in0=Pt[idx][:].rearrange("p (a b) -> p a b", a=4),
                                                        in1=cst_sb[:, mi, :].unsqueeze(1).to_broadcast([128, 4, 128]), op=ALU.mult), [b_Pt[idx], bc], [b_Pt[idx]])
            for idx, kb in enumerate(kbs):
                OP("pe", lambda e: e.matmul(po[:], lhsT=Vd[:, kb, g, :], rhs=Pt[idx][:], start=(idx == 0), stop=(idx == len(kbs) - 1)), [b_Vd, b_Pt[idx]], [b_po])
            for idx, kb in enumerate(kbs):
                OP("pe", lambda e: e.matmul(pd[:], lhsT=ones_bf[:], rhs=Pt[idx][:], start=(idx == 0), stop=(idx == len(kbs) - 1)), [b_ones, b_Pt[idx]], [b_pd])
            OP("dve", lambda e: e.tensor_tensor(out=den[:].rearrange("p (a b) -> p a b", a=4), in0=pd[:].rearrange("p (a b) -> p a b", a=4),
                                                in1=es[:, g * 4:(g + 1) * 4].unsqueeze(2).to_broadcast([128, 4, 128]), op=ALU.add), [b_pd, b_es], [b_den])
            OP("dve", lambda e: e.reciprocal(out=den[:], in_=den[:]), [b_den], [b_den])
            OP("dve", lambda e: e.tensor_tensor(out=onr[:], in0=po[:], in1=den[:], op=ALU.mult), [b_po, b_den], [b_onr])
            for j in range(2):
                OP("pool", lambda e: e.tensor_copy(out=ya[j][0:64, qs], in_=onr[0:64, j * 128:(j + 1) * 128]), [b_onr], [b_ya[j]])
                OP("pool", lambda e: e.tensor_copy(out=ya[j][64:128, qs], in_=onr[64:128, 256 + j * 128:256 + (j + 1) * 128]), [b_onr], [b_ya[j]])
        for j in range(2):
            at = 2 * g + j
            trow = tb + T_GAT + at
            P.dma("sp", lambda e: e.dma_start(out=gt[j][:], in_=projT_d[trow * 128:(trow + 1) * 128, :]), b_gt[j], reads=[c.b_projT[trow]], writes=[b_gt[j]])
            OP("act", lambda e: e.activation(out=gt[j][:], in_=gt[j][:], func=AF.Silu), [b_gt[j]], [b_gt[j]])
            OP("dve", lambda e: e.tensor_tensor(out=ya[j][:], in0=ya[j][:], in1=gt[j][:], op=ALU.mult), [b_ya[j], b_gt[j]], [b_ya[j]])
            P.dma("sp", lambda e: e.dma_start(out=mix_d[mrow + 512 + at * 128:mrow + 512 + (at + 1) * 128, :], in_=ya[j][:]), b_ya[j], reads=[b_ya[j]], writes=[c.b_mix], final=final)
    for i in range(4):
        j = i % 2
        trow = tb + T_GRW + i
        P.dma("sp", lambda e: e.dma_start(out=gt[j][:], in_=projT_d[trow * 128:(trow + 1) * 128, :]), b_gt[j], reads=[c.b_projT[trow]], writes=[b_gt[j]])
        P.dma("sp", lambda e: e.dma_start(out=ya[j][:], in_=yT_d[i * 128:(i + 1) * 128, :]), b_ya[j], reads=[(b_yT or c.b_yT)[i]], writes=[b_ya[j]])
        OP("act", lambda e: e.activation(out=gt[j][:], in_=gt[j][:], func=AF.Silu), [b_gt[j]], [b_gt[j]])
        OP("dve", lambda e: e.tensor_tensor(out=ya[j][:], in0=ya[j][:], in1=gt[j][:], op=ALU.mult), [b_ya[j], b_gt[j]], [b_ya[j]])
        P.dma("sp", lambda e: e.dma_start(out=mix_d[mrow + i * 128:mrow + (i + 1) * 128, :], in_=ya[j][:]), b_ya[j], reads=[b_ya[j]], writes=[c.b_mix], final=final)
    stC.close()


def build_B(final, ntok=2048):
    nc = bass.Bass("TRN2", target_bir_lowering=False)
    mixT_d = nc.dram_tensor("mixT", [D, ntok], F32, kind="ExternalInput").ap()
    x_d = nc.dram_tensor("x", [ntok, D], F32, kind="ExternalInput").ap()
    wo_d = nc.dram_tensor("wout", [D, D], F32, kind="ExternalInput").ap()
    fg_d = nc.dram_tensor("fg", [128, D], F32, kind="ExternalInput").ap()
    out_d = nc.dram_tensor("out", [ntok, D], F32, kind="ExternalOutput").ap()
    with ExitStack() as st:
        sb = lambda name, shape, dt: st.enter_context(nc.sbuf_tensor(name, shape, dt))
        ps = lambda name, shape, dt: st.enter_context(nc.psum_tensor(name, shape, dt))
        P = Prog(nc)
        B = lambda n: P.buf(n)
        mixb = sb("mixb", [128, KC, ntok], BF16)
        wob = sb("wob", [128, KC, D], BF16)
        fg = sb("fg_sb", [128, D], F32)
        xt = [sb(f"xt{i}", [128, D], F32) for i in range(2)]
        xo = [sb(f"xo{i}", [128, D], F32) for i in range(2)]
        junk = sb("junk", [128, D], BF16)
        stat = [sb(f"stat{i}", [128, 2], F32) for i in range(2)]
        epsc = sb("epsc", [128, 1], F32)
        pm = [ps(f"pm{i}", [128, 512], F32) for i in range(4)]
        b_mix = [B(f"mixb{k}") for k in range(KC)]
        b_wo = [B(f"wob{k}") for k in range(KC)]
        b_fg, b_junk, b_eps = B("fg"), B("junk"), B("eps")
        b_xt = [B(f"xt{i}") for i in range(2)]
        b_xo = [B(f"xo{i}") for i in range(2)]
        b_stat = [B(f"stat{i}") for i in range(2)]
        b_pm = [B(f"pm{i}") for i in range(4)]
        P.op("dve", lambda e: e.memset(epsc[:], 1e-6), writes=[b_eps])
        P.dma("sp", lambda e: e.dma_start(out=fg[:], in_=fg_d[:, :]), b_fg, writes=[b_fg])
        for kc in range(KC):
            P.dma("pool", lambda e: e.dma_start(out=mixb[:, kc, :], in_=mixT_d[kc * 128:(kc + 1) * 128, :]), b_mix[kc], writes=[b_mix[kc]])
            P.dma("pool", lambda e: e.dma_start(out=wob[:, kc, :], in_=wo_d[kc * 128:(kc + 1) * 128, :]), b_wo[kc], writes=[b_wo[kc]])
        nmm = 0
        for ti in range(ntok // 128):
            i = ti % 2
            ts = slice(ti * 128, (ti + 1) * 128)
            P.dma("sp", lambda e: e.dma_start(out=xt[i][:], in_=x_d[ts, :]), b_xt[i], writes=[b_xt[i]])
            for dg in range(D // 512):
                pi = nmm % 4
                nmm += 1
                ds_ = slice(dg * 512, (dg + 1) * 512)
                for kc in range(KC):
                    P.op("pe", lambda e: e.matmul(pm[pi][:], lhsT=mixb[:, kc, ts], rhs=wob[:, kc, ds_], start=(kc == 0), stop=(kc == KC - 1)),
                         reads=[b_mix[kc], b_wo[kc]], writes=[b_pm[pi]])
                P.op("dve", lambda e: e.tensor_tensor(out=xo[i][:, ds_], in0=pm[pi][:], in1=xt[i][:, ds_], op=ALU.add), reads=[b_pm[pi], b_xt[i]], writes=[b_xo[i]])
            if final:
                P.op("dve", lambda e: e.memset(stat[i][:], 0.0), writes=[b_stat[i]])
                P.op("act", lambda e: e.activation(out=junk[:], in_=xo[i][:], func=AF.Square, accum_out=stat[i][:, 0:1]), reads=[b_xo[i]], writes=[b_junk, b_stat[i]])
                P.op("act", lambda e: e.activation(out=stat[i][:, 1:2], in_=stat[i][:, 0:1], func=AF.Sqrt, scale=1.0 / D, bias=epsc[:]), reads=[b_stat[i], b_eps], writes=[b_stat[i]])
                P.op("dve", lambda e: e.reciprocal(out=stat[i][:, 1:2], in_=stat[i][:, 1:2]), reads=[b_stat[i]], writes=[b_stat[i]])
                P.op("dve", lambda e: e.scalar_tensor_tensor(out=xo[i][:], in0=xo[i][:], scalar=stat[i][:, 1:2], in1=fg[:], op0=ALU.mult, op1=ALU.mult),
                     reads=[b_xo[i], b_stat[i], b_fg], writes=[b_xo[i]])
            P.dma("sp", lambda e: e.dma_start(out=out_d[ts, :], in_=xo[i][:]), b_xo[i], reads=[b_xo[i]], final=True)
        P.emit(st)
        print("B", P.stats)
    return nc


def run_module(inputs, run_fn):
    x = np.ascontiguousarray(np.asarray(inputs["x"], np.float32))
    pos = np.asarray(inputs["positions"]).astype(np.int32)
    cst = make_consts()
    fg = np.ascontiguousarray(np.broadcast_to(np.asarray(inputs["final_g"], np.float32)[None, :], (128, D)))
    for l in range(2):
        ncA = build_A(stage=3, tt=T)
        in_maps = []
        for cid in range(8):
            b, hh = cid // 2, cid % 2
            in_maps.append({"x": np.ascontiguousarray(x[b]), "wsel": make_wsel(np.asarray(inputs["w_in"][l], np.float32), hh),
                            "par": make_par(inputs, l, hh), "cst": cst, "lora": make_lora(inputs, l, hh),
                            "pos": np.ascontiguousarray(pos[b:b + 1])})
        resA = run_fn(ncA, in_maps)
        ncB = build_B(final=(l == 1))
        in_maps = []
        wo = np.ascontiguousarray(np.asarray(inputs["w_out"][l], np.float32))
        for cid in range(8):
            b, th = cid // 2, cid % 2
            m0, m1 = resA[2 * b]["mix"], resA[2 * b + 1]["mix"]
            mixT = np.concatenate([m0[:512], m1[:512], m0[512:], m1[512:]], axis=0)
            in_maps.append({"mixT": np.ascontiguousarray(mixT[:, th * 2048:(th + 1) * 2048]),
                            "x": np.ascontiguousarray(x[b, th * 2048:(th + 1) * 2048]), "wout": wo, "fg": fg})
        resB = run_fn(ncB, in_maps)
        xn = np.empty_like(x)
        for cid in range(8):
            b, th = cid // 2, cid % 2
            xn[b, th * 2048:(th + 1) * 2048] = resB[cid]["out"]
        x = xn
    return x


NT2 = 2 * NT
NCOL2 = NT2 * 128


def emit_B(P, c, nc, sb, ps, mix_d, b_mixsrc, x_src, b_xsrc, wo_d, fg_sb, b_fg, epsc, b_eps, out_d, b_out, final, tt=T):
    B = lambda n: P.buf(n)
    HT = 2048
    mixb = sb("mixb", [128, KC, HT], BF16)
    wob = sb("wob", [128, KC, D], BF16)
    xt = [sb(f"bxt{i}", [128, D], F32) for i in range(2)]
    xo = [sb(f"bxo{i}", [128, D], F32) for i in range(2)]
    junk = sb("bjunk", [128, D], BF16)
    stat = [sb(f"bstat{i}", [128, 2], F32) for i in range(2)]
    pm = [ps(f"bpm{i}", [128, 512], F32) for i in range(4)]
    b_mix = [B(f"mixb{k}") for k in range(KC)]
    b_wo = [B(f"wob{k}") for k in range(KC)]
    b_junk = B("bjunk")
    b_xt = [B(f"bxt{i}") for i in range(2)]
    b_xo = [B(f"bxo{i}") for i in range(2)]
    b_stat = [B(f"bstat{i}") for i in range(2)]
    b_pm = [B(f"bpm{i}") for i in range(4)]
    for kc in range(KC):
        P.dma("pool", lambda e: e.dma_start(out=wob[:, kc, :], in_=wo_d[kc * 128:(kc + 1) * 128, :]), b_wo[kc], writes=[b_wo[kc]])
    nmm = 0
    for half in range(tt // HT):
        for kc in range(KC):
            P.dma("pool", lambda e: e.dma_start(out=mixb[:, kc, :], in_=mix_d[kc * 128:(kc + 1) * 128, half * HT:(half + 1) * HT]), b_mix[kc],
                  reads=[b_mixsrc], writes=[b_mix[kc]])
        for ti in range(HT // 128):
            i = ti % 2
            ts = slice(ti * 128, (ti + 1) * 128)
            gs = slice(half * HT + ti * 128, half * HT + (ti + 1) * 128)
            P.dma("sp", lambda e: e.dma_start(out=xt[i][:], in_=x_src[gs, :]), b_xt[i], reads=[b_xsrc], writes=[b_xt[i]])
            for dg in range(D // 512):
                pi = nmm % 4
                nmm += 1
                ds_ = slice(dg * 512, (dg + 1) * 512)
                for kc in range(KC):
                    P.op("pe", lambda e: e.matmul(pm[pi][:], lhsT=mixb[:, kc, ts], rhs=wob[:, kc, ds_], start=(kc == 0), stop=(kc == KC - 1)),
                         reads=[b_mix[kc], b_wo[kc]], writes=[b_pm[pi]])
                P.op("dve", lambda e: e.tensor_tensor(out=xo[i][:, ds_], in0=pm[pi][:], in1=xt[i][:, ds_], op=ALU.add), reads=[b_pm[pi], b_xt[i]], writes=[b_xo[i]])
            if final:
                P.op("dve", lambda e: e.memset(stat[i][:], 0.0), writes=[b_stat[i]])
                P.op("act", lambda e: e.activation(out=junk[:], in_=xo[i][:], func=AF.Square, accum_out=stat[i][:, 0:1]), reads=[b_xo[i]], writes=[b_junk, b_stat[i]])
                P.op("act", lambda e: e.activation(out=stat[i][:, 1:2], in_=stat[i][:, 0:1], func=AF.Sqrt, scale=1.0 / D, bias=epsc), reads=[b_stat[i], b_eps], writes=[b_stat[i]])
                P.op("dve", lambda e: e.reciprocal(out=stat[i][:, 1:2], in_=stat[i][:, 1:2]), reads=[b_stat[i]], writes=[b_stat[i]])
                P.op("dve", lambda e: e.scalar_tensor_tensor(out=xo[i][:], in0=xo[i][:], scalar=stat[i][:, 1:2], in1=fg_sb, op0=ALU.mult, op1=ALU.mult),
                     reads=[b_xo[i], b_stat[i], b_fg], writes=[b_xo[i]])
            P.dma("sp", lambda e: e.dma_start(out=out_d[gs, :], in_=xo[i][:]), b_xo[i], reads=[b_xo[i]], writes=[b_out], final=final)


def build_fused(tt=T):
    nc = bass.Bass("TRN2", target_bir_lowering=False)
    c = Ctx()
    x_d = nc.dram_tensor("x", [tt, D], F32, kind="ExternalInput").ap()
    wsel_d = [nc.dram_tensor(f"wsel{l}", [D, NCOL2], F32, kind="ExternalInput").ap() for l in range(2)]
    wo_d = [nc.dram_tensor(f"wout{l}", [D, D], F32, kind="ExternalInput").ap() for l in range(2)]
    par_d = nc.dram_tensor("par", [128, 4, NPAR], F32, kind="ExternalInput").ap()
    cst_d = nc.dram_tensor("cst", [128, NCST, 128], F32, kind="ExternalInput").ap()
    lora_d = nc.dram_tensor("lora", [4, 128, 4, 512], F32, kind="ExternalInput").ap()
    pos_d = nc.dram_tensor("pos", [1, tt], I32, kind="ExternalInput").ap()
    fg_d = nc.dram_tensor("fg", [128, D], F32, kind="ExternalInput").ap()
    out_d = nc.dram_tensor("out", [tt, D], F32, kind="ExternalOutput").ap()
    projT_d = nc.dram_tensor("projT", [NCOL2, tt], F32, kind="Internal").ap()
    yT_d = [nc.dram_tensor(f"yT{h}", [512, tt], F32, kind="Internal").ap() for h in range(2)]
    mix_d = nc.dram_tensor("mix", [2048, tt], F32, kind="Internal").ap()
    x1_d = nc.dram_tensor("x1", [tt, D], F32, kind="Internal").ap()
    with ExitStack() as st:
        sb = lambda name, shape, dt: st.enter_context(nc.sbuf_tensor(name, shape, dt))
        P = Prog(nc)
        c.b_projT = [P.buf(f"projT{j}") for j in range(NT2)]
        b_yT = [[P.buf(f"yT{h}_{j}") for j in range(4)] for h in range(2)]
        c.b_yT = b_yT[0]
        c.b_const = P.buf("const")
        c.b_mix = P.buf("mix")
        b_x1 = P.buf("x1")
        b_xin = P.buf("xin")
        b_out = P.buf("outd")
        scr = sb("scr", [128, 1], F32)
        par_all = sb("par_all", [128, 4, NPAR], F32)
        cst_sb = sb("cst_sb", [128, NCST, 128], F32)
        lora_sb = sb("lora_sb", [128, 4, 512], F32)
        fg_sb = sb("fg_sb", [128, D], F32)
        ident_b = sb("ident_b", [128, 128], BF16)
        b_par = P.buf("par"); b_lora = P.buf("lora"); b_fg = P.buf("fg")
        P.dma("sp", lambda e: e.dma_start(out=par_all[:], in_=par_d[:, :, :]), b_par, writes=[b_par])
        P.dma("sp", lambda e: e.dma_start(out=cst_sb[:], in_=cst_d[:, :, :]), c.b_const, writes=[c.b_const])
        P.dma("sp", lambda e: e.dma_start(out=fg_sb[:], in_=fg_d[:, :]), b_fg, writes=[b_fg])
        P.op("dve", lambda e: e.tensor_copy(out=ident_b[:], in_=cst_sb[:, 0, :]), reads=[c.b_const], writes=[c.b_const])
        for k in range(4):
            emit_derived(P, par_all[:, k, :], b_par)
        uid = [0]

        def scoped():
            es_ = ExitStack()
            uid[0] += 1
            u = uid[0]
            sbx = lambda name, shape, dt: es_.enter_context(nc.sbuf_tensor(f"{name}_u{u}", shape, dt))
            psx = lambda name, shape, dt: es_.enter_context(nc.psum_tensor(f"{name}_u{u}", shape, dt))
            return es_, sbx, psx

        bar = lambda: P.barrier(lambda e: e.memset(scr[:], 0.0))
        for l in range(2):
            x_src, b_xsrc = (x_d, b_xin) if l == 0 else (x1_d, b_x1)
            par0 = par_all[:, 2 * l, :]
            par = {"buf": b_par, "g": par0[:, 64:80], "eps": par0[:, 80:81]}
            es_, sbx, psx = scoped()
            emit_inproj(P, c, nc, sbx, psx, x_src, wsel_d[l], projT_d, par, ident_b[:], tt=tt, nt=NT2)
            es_.close()
            bar()
            for hh in range(2):
                pk = par_all[:, 2 * l + hh, :]
                P.dma("sp", lambda e: e.dma_start(out=lora_sb[:], in_=lora_d[2 * l + hh]), b_lora, writes=[b_lora])
                es_, sbx, psx = scoped()
                emit_wkv(P, c, nc, sbx, psx, projT_d, pk, b_par, cst_sb, lora_sb, b_lora, yT_d[hh], tt=tt, tb=hh * NT, b_yT=b_yT[hh])
                es_.close()
                bar()
                uid[0] += 1
                with ExitStack() as st3:
                    emit_attn(P, c, nc, st3, projT_d, yT_d[hh], pos_d, mix_d, pk, b_par, cst_sb, scr[:], tt=tt, tb=hh * NT, b_yT=b_yT[hh],
                              mrow=hh * 1024, final=False, sfx=f"_u{uid[0]}")
                bar()
            es_, sbx, psx = scoped()
            if l == 0:
                emit_B(P, c, nc, sbx, psx, mix_d, c.b_mix, x_src, b_xsrc, wo_d[l], fg_sb[:], b_fg, par0[:, 80:81], b_par, x1_d, b_x1, final=False, tt=tt)
            else:
                emit_B(P, c, nc, sbx, psx, mix_d, c.b_mix, x_src, b_xsrc, wo_d[l], fg_sb[:], b_fg, par0[:, 80:81], b_par, out_d, b_out, final=True, tt=tt)
            es_.close()
            bar()
        P.emit(st)
        print("fused", P.stats)
    return nc


def perm_wout(wo):
    return np.ascontiguousarray(np.concatenate([wo[0:512], wo[1024:1536], wo[512:1024], wo[1536:2048]], axis=0))


def make_inputs(inputs):
    x = np.ascontiguousarray(np.asarray(inputs["x"], np.float32))
    pos = np.asarray(inputs["positions"]).astype(np.int32)
    cst = make_consts()
    fg = np.ascontiguousarray(np.broadcast_to(np.asarray(inputs["final_g"], np.float32)[None, :], (128, D)))
    wsel = [np.concatenate([make_wsel(np.asarray(inputs["w_in"][l], np.float32), hh) for hh in range(2)], axis=1) for l in range(2)]
    wo = [perm_wout(np.asarray(inputs["w_out"][l], np.float32)) for l in range(2)]
    par = np.stack([make_par(inputs, l, hh) for l in range(2) for hh in range(2)], axis=1)
    lora = np.stack([make_lora(inputs, l, hh) for l in range(2) for hh in range(2)], axis=0)
    maps = []
    for cid in range(8):
        b = cid // 2
        maps.append({"x": x[b], "wsel0": wsel[0], "wsel1": wsel[1], "wout0": wo[0], "wout1": wo[1], "par": np.ascontiguousarray(par),
                     "cst": cst, "lora": np.ascontiguousarray(lora), "pos": np.ascontiguousarray(pos[b:b + 1]), "fg": fg})
    return maps


def run_fused(inputs, run_fn):
    nc = build_fused()
    res = run_fn(nc, make_inputs(inputs))
    out = np.empty((4, T, D), np.float32)
    for b in range(4):
        out[b, :2048] = res[2 * b]["out"][:2048]
        out[b, 2048:] = res[2 * b + 1]["out"][2048:]
    return out


def kernel(**inputs):
    def run_fn(nc, maps):
        return run_bass_kernel_spmd(nc, maps, core_ids=list(range(8))).results
    return run_fused(inputs, run_fn).astype(np.float32)
```
